# Optimizing a Trainium2 kernel written in Bass

```python
import math
import jax, jax.numpy as jnp
from jax import lax
import numpy as np

D_MODEL = 2048
BATCH = 16
SEQ = 2048
DEPTH = 1

D_MLSTM = D_MODEL // 2
N_MLSTM_HEADS = 4
DV_HEAD = D_MLSTM // N_MLSTM_HEADS
DQK_HEAD = DV_HEAD // 2
D_QK = N_MLSTM_HEADS * DQK_HEAD
MLSTM_CHUNK = 128
CONV_WIDTH = 4
D_GMLP = D_MODEL - D_MLSTM
N_GMLP_GROUPS = 8
GMLP_GROUP = D_GMLP // N_GMLP_GROUPS
SPATIAL_CHUNK = 128
D_FF = 5632
N_SUBLAYERS = 3
N_MOD = 3
EPS = 1e-6
D_IN = 2 * D_QK + 2 * D_MLSTM + 2 * N_MLSTM_HEADS + 2 * D_GMLP
SPLITS = tuple(np.cumsum([D_QK, D_QK, D_MLSTM, D_MLSTM, N_MLSTM_HEADS, N_MLSTM_HEADS, D_GMLP])[:].tolist())

kernel_name = "hymba_style_mlstm_gmlp_macaron_adaln"


def rms_norm(x, g):
    xf = x.astype(jnp.float32)
    y = xf * lax.rsqrt(jnp.mean(xf * xf, axis=-1, keepdims=True) + EPS)
    return (y * g.astype(jnp.float32)).astype(x.dtype)


def layer_norm(x, g, b):
    xf = x.astype(jnp.float32)
    mu = jnp.mean(xf, axis=-1, keepdims=True)
    var = jnp.mean(jnp.square(xf - mu), axis=-1, keepdims=True)
    y = (xf - mu) * lax.rsqrt(var + EPS)
    return (y * g.astype(jnp.float32) + b.astype(jnp.float32)).astype(x.dtype)


def causal_dwconv(x, w, b):
    s = x.shape[1]
    xp = jnp.pad(x, ((0, 0), (CONV_WIDTH - 1, 0), (0, 0)))
    y = sum(w[j] * xp[:, j:j + s] for j in range(CONV_WIDTH))
    return y + b


def swiglu(h, w_gate, w_up, w_down):
    return (jax.nn.silu(h @ w_gate) * (h @ w_up)) @ w_down


def mlstm_chunkwise(q, k, v, i_pre, f_pre):
    b_, s_, h_, _ = q.shape
    nc, L = s_ // MLSTM_CHUNK, MLSTM_CHUNK

    def to_chunks(t):
        return t.astype(jnp.float32).reshape(b_, nc, L, h_, t.shape[-1]).transpose(0, 3, 1, 2, 4)

    def gate_chunks(t):
        return t.astype(jnp.float32).reshape(b_, nc, L, h_).transpose(0, 3, 1, 2)

    qc, kc, vc = to_chunks(q), to_chunks(k), to_chunks(v)
    ig = gate_chunks(i_pre)
    logf = jax.nn.log_sigmoid(gate_chunks(f_pre))
    bcum = jnp.cumsum(logf, axis=-1)
    b_last = bcum[..., -1]

    causal = jnp.tril(jnp.ones((L, L), dtype=bool))
    dmat = jnp.where(causal, bcum[..., :, None] - bcum[..., None, :] + ig[..., None, :], -jnp.inf)
    m_intra = jnp.max(dmat, axis=-1)

    w_state = b_last[..., None] - bcum + ig
    m_loc = jnp.max(w_state, axis=-1)
    e_state = jnp.exp(w_state - m_loc[..., None])
    c_inc = jnp.einsum('bhnl,bhnle,bhnld->bhned', e_state, vc, kc)
    n_inc = jnp.einsum('bhnl,bhnld->bhnd', e_state, kc)

    def step(carry, xs):
        c_st, n_st, m_st = carry
        c_x, n_x, m_x, b_x = xs
        m_new = jnp.maximum(b_x + m_st, m_x)
        a = jnp.exp(b_x + m_st - m_new)
        s = jnp.exp(m_x - m_new)
        c_new = a[..., None, None] * c_st + s[..., None, None] * c_x
        n_new = a[..., None] * n_st + s[..., None] * n_x
        return (c_new, n_new, m_new), (c_st, n_st, m_st)

    init = (jnp.zeros((b_, h_, DV_HEAD, DQK_HEAD), jnp.float32),
            jnp.zeros((b_, h_, DQK_HEAD), jnp.float32),
            jnp.zeros((b_, h_), jnp.float32))
    xs = (jnp.moveaxis(c_inc, 2, 0), jnp.moveaxis(n_inc, 2, 0),
          jnp.moveaxis(m_loc, 2, 0), jnp.moveaxis(b_last, 2, 0))
    _, (c_prev, n_prev, m_prev) = lax.scan(step, init, xs)
    c_prev = jnp.moveaxis(c_prev, 0, 2)
    n_prev = jnp.moveaxis(n_prev, 0, 2)
    m_prev = jnp.moveaxis(m_prev, 0, 2)

    m_inter = bcum + m_prev[..., None]
    m_t = jnp.maximum(m_inter, m_intra)
    inter_scale = jnp.exp(m_inter - m_t)
    p = jnp.exp(dmat - m_t[..., None])
    s_qk = jnp.einsum('bhnld,bhnsd->bhnls', qc, kc) * p
    num = (inter_scale[..., None] * jnp.einsum('bhnld,bhned->bhnle', qc, c_prev)
           + jnp.einsum('bhnls,bhnse->bhnle', s_qk, vc))
    den = inter_scale * jnp.einsum('bhnld,bhnd->bhnl', qc, n_prev) + jnp.sum(s_qk, axis=-1)
    hc = num / jnp.maximum(jnp.abs(den), jnp.exp(-m_t))[..., None]
    return hc.transpose(0, 2, 3, 1, 4).reshape(b_, s_, h_, DV_HEAD)


def spatial_gating(u, v, ln_g, ln_b, w_sp, b_sp):
    b_, s_, _ = v.shape
    nc, L = s_ // SPATIAL_CHUNK, SPATIAL_CHUNK
    v = layer_norm(v, ln_g, ln_b).reshape(b_, nc, L, N_GMLP_GROUPS, GMLP_GROUP)
    w_causal = w_sp * jnp.tril(jnp.ones((L, L), dtype=w_sp.dtype))
    z = jnp.einsum('gts,bnsgc->bntgc', w_causal, v) + b_sp.T[None, None, :, :, None]
    return u * z.reshape(b_, s_, D_GMLP)


def setup_inputs(seed: int = 0) -> dict:
    key = jax.random.key(seed)
    ks = jax.random.split(key, 24)
    f32 = jnp.float32
    nrm = lambda k, shape, scale: jax.random.normal(k, shape, f32) * scale
    return {
        "x": nrm(ks[0], (BATCH, SEQ, D_MODEL), 1.0),
        "c": nrm(ks[1], (BATCH, D_MODEL), 1.0),
        "w_ada": nrm(ks[2], (DEPTH, D_MODEL, N_SUBLAYERS * N_MOD * D_MODEL), 0.5 * D_MODEL ** -0.5),
        "b_ada": nrm(ks[3], (DEPTH, N_SUBLAYERS * N_MOD * D_MODEL), 0.01),
        "g_pre": 1.0 + nrm(ks[4], (DEPTH, N_SUBLAYERS, D_MODEL), 0.05),
        "g_post": 1.0 + nrm(ks[5], (DEPTH, N_SUBLAYERS, D_MODEL), 0.05),
        "w_ff_gate": nrm(ks[6], (DEPTH, 2, D_MODEL, D_FF), D_MODEL ** -0.5),
        "w_ff_up": nrm(ks[7], (DEPTH, 2, D_MODEL, D_FF), D_MODEL ** -0.5),
        "w_ff_down": nrm(ks[8], (DEPTH, 2, D_FF, D_MODEL), D_FF ** -0.5),
        "w_in": nrm(ks[9], (DEPTH, D_MODEL, D_IN), D_MODEL ** -0.5),
        "conv_w": nrm(ks[10], (DEPTH, CONV_WIDTH, 2 * D_QK), CONV_WIDTH ** -0.5),
        "conv_b": nrm(ks[11], (DEPTH, 2 * D_QK), 0.01),
        "b_igate": nrm(ks[12], (DEPTH, N_MLSTM_HEADS), 0.1),
        "b_fgate": jnp.linspace(3.0, 6.0, N_MLSTM_HEADS, dtype=f32)[None, :] + nrm(ks[13], (DEPTH, N_MLSTM_HEADS), 0.1),
        "g_mhnorm": 1.0 + nrm(ks[14], (DEPTH, N_MLSTM_HEADS, DV_HEAD), 0.05),
        "gmlp_ln_g": 1.0 + nrm(ks[15], (DEPTH, D_GMLP), 0.05),
        "gmlp_ln_b": nrm(ks[16], (DEPTH, D_GMLP), 0.01),
        "w_spatial": nrm(ks[17], (DEPTH, N_GMLP_GROUPS, SPATIAL_CHUNK, SPATIAL_CHUNK), SPATIAL_CHUNK ** -0.5),
        "b_spatial": 1.0 + nrm(ks[18], (DEPTH, N_GMLP_GROUPS, SPATIAL_CHUNK), 0.05),
        "w_out": nrm(ks[19], (DEPTH, D_MODEL, D_MODEL), D_MODEL ** -0.5),
    }


def reference(x, c, w_ada, b_ada, g_pre, g_post, w_ff_gate, w_ff_up, w_ff_down, w_in, conv_w, conv_b,
              b_igate, b_fgate, g_mhnorm, gmlp_ln_g, gmlp_ln_b, w_spatial, b_spatial, w_out):
    b_, s_, _ = x.shape
    for l in range(DEPTH):
        mod = (jax.nn.silu(c) @ w_ada[l] + b_ada[l]).reshape(b_, N_SUBLAYERS, N_MOD, D_MODEL)

        def sublayer(x_res, j, fn, coef):
            shift, scale, gate = mod[:, j, 0], mod[:, j, 1], mod[:, j, 2]
            h = rms_norm(x_res, g_pre[l, j]) * (1.0 + scale[:, None, :]) + shift[:, None, :]
            y = rms_norm(fn(h), g_post[l, j])
            return x_res + coef * gate[:, None, :] * y

        def ffn(idx):
            return lambda h: swiglu(h, w_ff_gate[l, idx], w_ff_up[l, idx], w_ff_down[l, idx])

        def mixer(h):
            proj = h @ w_in[l]
            q, k, v, o, ig, fg, u, gv = jnp.split(proj, SPLITS, axis=-1)
            qk = jax.nn.silu(causal_dwconv(jnp.concatenate([q, k], axis=-1), conv_w[l], conv_b[l]))
            q, k = qk[..., :D_QK], qk[..., D_QK:]
            q = q.reshape(b_, s_, N_MLSTM_HEADS, DQK_HEAD)
            k = k.reshape(b_, s_, N_MLSTM_HEADS, DQK_HEAD) * (DQK_HEAD ** -0.5)
            v = v.reshape(b_, s_, N_MLSTM_HEADS, DV_HEAD)
            hm = mlstm_chunkwise(q, k, v, ig + b_igate[l], fg + b_fgate[l])
            hm = rms_norm(hm, g_mhnorm[l]).astype(x.dtype).reshape(b_, s_, D_MLSTM)
            hm = jax.nn.sigmoid(o) * hm
            z = spatial_gating(jax.nn.gelu(u), jax.nn.gelu(gv), gmlp_ln_g[l], gmlp_ln_b[l],
                               w_spatial[l], b_spatial[l])
            return jnp.concatenate([hm, z], axis=-1) @ w_out[l]

        x = sublayer(x, 0, ffn(0), 0.5)
        x = sublayer(x, 1, mixer, 1.0)
        x = sublayer(x, 2, ffn(1), 0.5)
    return x
```

```python
import contextlib
import numpy as np
import concourse.bass as bass
import concourse.mybir as mybir
from concourse.bass_utils import run_bass_kernel_spmd

F32 = mybir.dt.float32
BF16 = mybir.dt.bfloat16
AF = mybir.ActivationFunctionType
ALU = mybir.AluOpType
AX = mybir.AxisListType

D = 2048
KC = 16
S = 2048
T = 512
NCH = T // 128
DFF = 5632
FC = 44
NSEQ = 2
NTOK = NSEQ * S
NT = NTOK // T
TPS = S // T
EPS = 1e-6
NH = 4
DQK = 128
DV = 256
VW = 258
GU_BLK = 22
DN_BLK = 16
ADA_BLK = 36
WIN_BLK = 10
WOUT_BLK = 4
SLOT = 8192
NSLOT = 2
RING_PIECE = 8192

ENGS = ("pe", "act", "dve", "pool", "sp")


class Buf:
    __slots__ = ("name", "w", "r")

    def __init__(self, name):
        self.name = name
        self.w = None
        self.r = []


class Chan:
    def __init__(self, key):
        self.key = key
        self.n = 0


class Prog:
    def __init__(self):
        self.q = {e: [] for e in ENGS}
        self.cnt = {e: 0 for e in ENGS}
        self.waited = {e: {} for e in ENGS}
        self.semnames = ["E_" + e for e in ENGS]
        self.sems = {}
        self.log = []

    def chan(self, name):
        key = "C_" + name
        self.semnames.append(key)
        return Chan(key)

    def _need(self, eng, ev, raw):
        sem, val, origin = ev
        if origin == eng and not raw:
            return
        if self.waited[eng].get(sem, 0) >= val:
            return
        self.waited[eng][sem] = val
        self.log.append((eng, "wait", sem, val))
        self.q[eng].append(lambda e, s=sem, v=val: e.wait_ge(self.sems[s], v))

    def _deps(self, eng, reads, writes):
        for b in reads:
            if b.w is not None:
                self._need(eng, b.w, True)
        for b in writes:
            if b.w is not None:
                self._need(eng, b.w, False)
            for ev in b.r:
                self._need(eng, ev, False)

    def op(self, eng, fns, reads=(), writes=()):
        if not isinstance(fns, (list, tuple)):
            fns = [fns]
        self._deps(eng, reads, writes)
        self.cnt[eng] += 1
        key = "E_" + eng
        ev = (key, self.cnt[eng], eng)
        self.log.append((eng, "op", self.cnt[eng], [b.name for b in reads], [b.name for b in writes]))
        for f in fns[:-1]:
            self.q[eng].append(f)
        last = fns[-1]
        self.q[eng].append(lambda e, f=last, k=key: f(e).then_inc(self.sems[k], 1))
        for b in reads:
            b.r.append(ev)
        for b in writes:
            b.w = ev
            b.r = []
        return ev

    def dma(self, eng, chan, out_ap, in_ap, reads=(), writes=()):
        self._deps(eng, reads, writes)
        chan.n += 1
        ev = (chan.key, 16 * chan.n, "dma")
        self.q[eng].append(
            lambda e, o=out_ap, i=in_ap, k=chan.key: e.dma_start(out=o, in_=i).then_inc(self.sems[k], 16)
        )
        for b in reads:
            b.r.append(ev)
        for b in writes:
            b.w = ev
            b.r = []
        return ev

    def wait_event(self, eng, ev):
        self._need(eng, ev, True)


def build_nc(mixer=True, do_ffn=True, ntiles=NT, ada=True, stage=3, dbg_nch=NCH, mstage=8):
    nc = bass.Bass("TRN2", target_bir_lowering=False)
    P = Prog()

    def din(name, shape):
        return nc.dram_tensor(name, list(shape), F32, kind="ExternalInput").ap()

    xT_d = din("xT", (D, NTOK))
    c_d = din("c_l", (128, KC, NSEQ))
    wada_d = din("wada", (ADA_BLK, 128, KC * 512))
    bada_d = din("bada", (128, 144))
    gpre_d = din("gpre", (128, 3 * KC))
    gpost_d = din("gpost", (128, 3 * KC))
    gu_d = [din("gu%d" % i, (GU_BLK, 128, 2 * KC * 256)) for i in range(2)]
    dn_d = [din("dn%d" % i, (DN_BLK, 128, FC * 128)) for i in range(2)]
    win_d = din("win", (WIN_BLK, 128, KC * 512))
    wgate_d = din("wgate", (128, KC * 8))
    wout_d = din("wout", (WOUT_BLK, 128, KC * 512))
    convw_d = din("convw", (128, 8 * 4))
    convb_d = din("convb", (128, 8))
    big_d = din("big", (4, 1))
    bfg_d = din("bfg", (4, 1))
    gmh_d = din("gmh", (1, 1024))
    lng_d = din("lng", (1, 1024))
    lnb_d = din("lnb", (1, 1024))
    wsp_d = din("wsp", (128, 8 * 128))
    bsp_d = din("bsp", (1, 1024))
    ident_d = din("ident", (128, 128))
    mask4_d = din("mask4", (128, 512))
    tril_d = din("tril", (128, 128))
    reset_d = din("resetm", (4, 512))
    gmhbc_d = din("gmhbc", (128, 1024))
    lngbc_d = din("lngbc", (128, 1024))
    lnbbc_d = din("lnbbc", (128, 1024))
    out_d = nc.dram_tensor("outT", [D, NTOK], F32, kind="ExternalOutput").ap()

    es = contextlib.ExitStack()
    with es:
        def sb(name, shape, dt=F32):
            return es.enter_context(nc.sbuf_tensor(name, list(shape), dt))

        def psum(name, shape, dt=F32):
            return es.enter_context(nc.psum_tensor(name, list(shape), dt))

        XT = sb("XT", [128, KC * T])
        HT = sb("HT", [128, KC * T], BF16)
        ACTB = sb("ACTB", [128, FC * T], BF16)
        YT = sb("YT", [128, KC * T])
        RING = [sb("RING%d" % i, [128, SLOT], BF16) for i in range(NSLOT)]
        SG = [sb("SG%d" % i, [128, T]) for i in range(2)]
        RSTD = sb("RSTD", [128, T])
        RT = sb("RT", [128, T])
        ONESB = sb("ONESB", [128, 128], BF16)
        MODT = sb("MODT", [128, 144 * NSEQ])
        BADA = sb("BADA", [128, 144])
        GPRE = sb("GPRE", [128, 3 * KC])
        GPOST = sb("GPOST", [128, 3 * KC])
        PA = sb("PA", [128, NSEQ * 3 * KC])
        PSH = sb("PSH", [128, NSEQ * 3 * KC])
        PCG = sb("PCG", [128, NSEQ * 3 * KC])
        CL = sb("CL", [128, KC * NSEQ])
        SCT = sb("SCT", [128, KC * NSEQ], BF16)

        if mixer:
            WGATE = sb("WGATE", [128, KC * 8], BF16)
            CONVW = sb("CONVW", [128, 32]); CONVB = sb("CONVB", [128, 8])
            BIG = sb("BIG", [4, 1]); BFG = sb("BFG", [4, 1]); NBFG = sb("NBFG", [4, 1])
            GMHB = sb("GMHB", [128, 1024], BF16); LNGB = sb("LNGB", [128, 1024], BF16); LNBB = sb("LNBB", [128, 1024], BF16)
            WCT = sb("WCT", [128, 1024], BF16)
            BSPB = sb("BSPB", [1, 1024], BF16)
            ONESF = sb("ONESF", [4, 128])
            MASK4 = sb("MASK4", [128, 512], BF16)
            TRIL = sb("TRIL", [128, 128])
            IDENTB = sb("IDENTB", [128, 128], BF16); IDENTF = sb("IDENTF", [128, 128])
            RESETM = sb("RESETM", [4, 512])
            VP = sb("VP", [128, 16 * VW], BF16)
            STATE = sb("STATE", [128, 4 * 257]); STBF = sb("STBF", [128, 4 * VW], BF16)
            QKH = sb("QKH", [128, 24])
            MCUR = sb("MCUR", [4, 1]); MM_ = sb("MM", [4, 4]); AL = sb("AL", [4, 4]); ALPHA = sb("ALPHA", [4, 4]); GMAX = sb("GMAX", [4, 4])
            ALD = sb("ALD", [4, 16]); ER = sb("ER", [128, 32]); ALB = sb("ALB", [128, 16])
            KTOK = sb("KTOK", [128, 512], BF16); SPT = sb("SPT", [128, 512], BF16); HM = sb("HM", [128, 1024], BF16)
            SMALL = sb("SMALL", [128, 64])
        PSB = psum("PSB", [128, 1024], BF16)
        PS = [None] + [psum("PS%d" % i, [128, 512]) for i in range(1, 8)]

        xt = [Buf("xt%d" % i) for i in range(KC)]
        ht = [Buf("ht%d" % i) for i in range(KC)]
        actb = [Buf("actb%d" % i) for i in range(FC)]
        yt = [Buf("yt%d" % i) for i in range(KC)]
        ringb = [Buf("ring%d" % i) for i in range(NSLOT)]
        ringc = [P.chan("ring%d" % i) for i in range(NSLOT)]
        sgb = [Buf("sg0"), Buf("sg1")]
        rstdb, rtb = Buf("rstd"), Buf("rt")
        psb = [Buf("ps%d" % i) for i in range(8)]
        parb = Buf("params")
        cpar = P.chan("par")
        cxin = P.chan("xin")
        cxout = P.chan("xout")
        ring_i = [0]

        def XTc(k):
            return XT[:, k * T:(k + 1) * T]

        def HTc(k):
            return HT[:, k * T:(k + 1) * T]

        def ACc(k):
            return ACTB[:, k * T:(k + 1) * T]

        def YTc(k):
            return YT[:, k * T:(k + 1) * T]

        def ring_load(src_ap, nelem):
            i = ring_i[0] % NSLOT
            ring_i[0] += 1
            for o in range(0, nelem, RING_PIECE):
                n = min(RING_PIECE, nelem - o)
                P.dma("pool", ringc[i], RING[i][:, o:o + n], src_ap[:, o:o + n], writes=[ringb[i]] if o == 0 else [])
            ringb[i].w = (ringc[i].key, 16 * ringc[i].n, "dma")
            return i

        def mm(out, lhsT, rhs, start, stop):
            return lambda e: e.matmul(out, lhsT, rhs, start=start, stop=stop)

        par_loads = [(CL[:], c_d.rearrange("p k b -> p (k b)")), (BADA[:], bada_d), (GPRE[:], gpre_d), (GPOST[:], gpost_d)]
        for o, i in par_loads:
            P.dma("sp", cpar, o, i, writes=[])
        parb.w = (cpar.key, 16 * cpar.n, "dma")
        onesb = Buf("ones")
        P.op("dve", lambda e: e.memset(ONESB[:], 1.0), writes=[onesb])
        sctb = Buf("sct")
        P.op("act", lambda e: e.activation(out=SCT[:], in_=CL[:], func=AF.Silu), reads=[parb], writes=[sctb])
        SCT3 = SCT[:].rearrange("p (k b) -> p k b", b=NSEQ)
        for blk in range(ADA_BLK if ada else 0):
            si = ring_load(wada_d[blk], KC * 512)
            wv = RING[si][:, 0:KC * 512].rearrange("p (k c) -> p k c", k=KC)
            for j in range(4):
                fchunk = blk * 4 + j
                fns = [mm(PS[7][:, fchunk * NSEQ:(fchunk + 1) * NSEQ], wv[:, kc, j * 128:(j + 1) * 128], SCT3[:, kc, :], kc == 0, kc == KC - 1)
                       for kc in range(KC)]
                P.op("pe", fns, reads=[ringb[si], sctb], writes=[psb[7]])
        modb = Buf("modt")
        MODT3 = MODT[:].rearrange("p (m b) -> p m b", b=NSEQ)
        P.op("dve", lambda e: e.tensor_tensor(out=MODT3, in0=PS[7][:, 0:144 * NSEQ].rearrange("p (m b) -> p m b", b=NSEQ),
                                              in1=BADA[:].unsqueeze(2).to_broadcast([128, 144, NSEQ]), op=ALU.add),
             reads=[psb[7], parb], writes=[modb])
        derb = Buf("derived")
        coefs = [0.5, 1.0, 0.5]
        for b in range(NSEQ):
            for j in range(3):
                o = (b * 3 + j) * KC
                sh = MODT3[:, (j * 3 + 0) * KC:(j * 3 + 1) * KC, b]
                sc = MODT3[:, (j * 3 + 1) * KC:(j * 3 + 2) * KC, b]
                gt = MODT3[:, (j * 3 + 2) * KC:(j * 3 + 3) * KC, b]
                P.op("dve", lambda e, o=o, sc=sc, j=j: e.scalar_tensor_tensor(out=PA[:, o:o + KC], in0=sc, scalar=1.0, in1=GPRE[:, j * KC:(j + 1) * KC],
                                                                          op0=ALU.add, op1=ALU.mult), reads=[modb, parb], writes=[derb])
                P.op("dve", lambda e, o=o, sh=sh: e.tensor_copy(out=PSH[:, o:o + KC], in_=sh), reads=[modb], writes=[derb])
                P.op("dve", lambda e, o=o, gt=gt, j=j: e.scalar_tensor_tensor(out=PCG[:, o:o + KC], in0=gt, scalar=coefs[j], in1=GPOST[:, j * KC:(j + 1) * KC],
                                                                          op0=ALU.mult, op1=ALU.mult), reads=[modb, parb], writes=[derb])

        def rstd_from_sq():
            fns = [mm(PS[7][:, :], ONESB[:, :], HTc(kc), kc == 0, kc == KC - 1) for kc in range(KC)]
            P.op("pe", fns, reads=ht + [onesb], writes=[psb[7]])
            P.op("act", lambda e: e.activation(out=RT[:], in_=PS[7][:, :], func=AF.Sqrt, bias=EPS, scale=1.0 / D), reads=[psb[7]], writes=[rtb])
            P.op("dve", lambda e: e.reciprocal(out=RSTD[:], in_=RT[:]), reads=[rtb], writes=[rstdb])

        def prenorm(b, j):
            o = (b * 3 + j) * KC
            for kc in range(KC):
                P.op("act", lambda e, kc=kc: e.activation(out=HTc(kc), in_=XTc(kc), func=AF.Square), reads=[xt[kc]], writes=[ht[kc]])
            rstd_from_sq()
            for kc in range(KC):
                P.op("dve", lambda e, kc=kc: e.scalar_tensor_tensor(out=YTc(kc), in0=XTc(kc), scalar=PA[:, o + kc:o + kc + 1], in1=RSTD[:],
                                                                    op0=ALU.mult, op1=ALU.mult), reads=[xt[kc], rstdb, derb], writes=[yt[kc]])
                P.op("act", lambda e, kc=kc: e.activation(out=HTc(kc), in_=YTc(kc), func=AF.Identity, bias=PSH[:, o + kc:o + kc + 1], scale=1.0),
                     reads=[yt[kc], derb], writes=[ht[kc]])

        def postnorm(b, j):
            o = (b * 3 + j) * KC
            rstd_from_sq()
            for kc in range(KC):
                P.op("dve", lambda e, kc=kc: e.scalar_tensor_tensor(out=YTc(kc), in0=YTc(kc), scalar=PCG[:, o + kc:o + kc + 1], in1=RSTD[:],
                                                                    op0=ALU.mult, op1=ALU.mult), reads=[yt[kc], rstdb, derb], writes=[yt[kc]])
                P.op("dve", lambda e, kc=kc: e.tensor_tensor(out=XTc(kc), in0=XTc(kc), in1=YTc(kc), op=ALU.add), reads=[xt[kc], yt[kc]], writes=[xt[kc]])

        def evac_y(dchunk, bank):
            P.op("dve", lambda e: e.tensor_copy(out=YTc(dchunk), in_=PS[bank][:, :]), reads=[psb[bank]], writes=[yt[dchunk]])
            P.op("act", lambda e: e.activation(out=HTc(dchunk), in_=YTc(dchunk), func=AF.Square), reads=[yt[dchunk]], writes=[ht[dchunk]])

        def ffn(idx):
            for blk in range(GU_BLK):
                si = ring_load(gu_d[idx][blk], 2 * KC * 256)
                wv = RING[si][:, 0:2 * KC * 256].rearrange("p (g k c) -> p g k c", g=2, k=KC)
                for j in range(2):
                    f = blk * 2 + j
                    bg, bu = 1 + (f % 2) * 2, 2 + (f % 2) * 2
                    for g, bank in ((0, bg), (1, bu)):
                        fns = [mm(PS[bank][:, :], wv[:, g, kc, j * 128:(j + 1) * 128], HTc(kc), kc == 0, kc == KC - 1) for kc in range(KC)]
                        P.op("pe", fns, reads=[ringb[si]] + ht, writes=[psb[bank]])
                    s = f % 2
                    P.op("act", lambda e, s=s, bg=bg: e.activation(out=SG[s][:], in_=PS[bg][:, :], func=AF.Silu), reads=[psb[bg]], writes=[sgb[s]])
                    P.op("dve", lambda e, s=s, bu=bu, f=f: e.tensor_tensor(out=ACc(f), in0=PS[bu][:, :], in1=SG[s][:], op=ALU.mult),
                         reads=[psb[bu], sgb[s]], writes=[actb[f]])
            for blk in range(DN_BLK if stage >= 3 else 0):
                si = ring_load(dn_d[idx][blk], FC * 128)
                wv = RING[si][:, 0:FC * 128].rearrange("p (k c) -> p k c", k=FC)
                bank = 5 + blk % 2
                fns = [mm(PS[bank][:, :], wv[:, kc, :], ACc(kc), kc == 0, kc == FC - 1) for kc in range(FC)]
                P.op("pe", fns, reads=[ringb[si]] + actb, writes=[psb[bank]])
                evac_y(blk, bank)


        if mixer:
            cpar2 = P.chan("par2")
            cpar3 = P.chan("par3")
            mconst = Buf("mconst")
            for o, i in [(CONVW[:], convw_d), (CONVB[:], convb_d), (BIG[:], big_d), (BFG[:], bfg_d), (TRIL[:], tril_d), (IDENTF[:], ident_d),
                         (RESETM[:], reset_d), (YT[:, 0:1024], wsp_d)]:
                P.dma("sp", cpar3, o, i, writes=[])
            mconst.w = (cpar3.key, 16 * cpar3.n, "dma")
            for kc in range(2):
                yt[kc].w = mconst.w
            for o, i in [(WGATE[:], wgate_d), (GMHB[:], gmhbc_d), (LNGB[:], lngbc_d), (LNBB[:], lnbbc_d), (BSPB[:], bsp_d), (MASK4[:], mask4_d),
                         (IDENTB[:], ident_d)]:
                P.dma("pool", cpar2, o, i, writes=[])
            mconst2 = Buf("mconst2")
            mconst2.w = (cpar2.key, 16 * cpar2.n, "dma")
            mc = [mconst, mconst2]
            P.op("dve", lambda e: e.memset(ONESF[:], 1.0), writes=[mconst])
            P.op("dve", lambda e: e.tensor_scalar(out=NBFG[:], in0=BFG[:], scalar1=-1.0, scalar2=None, op0=ALU.mult), reads=[mconst], writes=[mconst])
            wctb = Buf("wct")
            WSP3 = YT[:, 0:1024].rearrange("p (g s) -> p g s", g=8)
            P.op("dve", lambda e: e.tensor_tensor(out=WSP3, in0=WSP3, in1=TRIL[:].unsqueeze(1).to_broadcast([128, 8, 128]), op=ALU.mult),
                 reads=[yt[0], yt[1], mconst], writes=[yt[0], yt[1]])
            for g in range(8):
                bank = 1 + g // 4
                P.op("pe", mm(PS[bank][:, (g % 4) * 128:(g % 4 + 1) * 128], WSP3[:, g, :], IDENTF[:], True, True), reads=[yt[0], yt[1], mconst], writes=[psb[bank]])
            for hb in range(2):
                P.op("act", lambda e, hb=hb: e.activation(out=WCT[:, hb * 512:(hb + 1) * 512], in_=PS[1 + hb][:, :], func=AF.Identity),
                     reads=[psb[1 + hb]], writes=[wctb])
            WCT3 = WCT[:].rearrange("p (g t) -> p g t", g=8)
            WG3 = WGATE[:].rearrange("p (k c) -> p k c", k=KC)
            scrA = Buf("scrA")
            QKP3 = ACTB[:, 24 * T:24 * T + 8240].bitcast(F32).rearrange("p (j t) -> p j t", j=8)
            GG3 = ACTB[:, 24 * T:24 * T + 8192].bitcast(F32).rearrange("p (n f) -> p n f", n=4)
            QKH3 = QKH[:].rearrange("p (j t) -> p j t", j=8)
            VP4 = VP[:].rearrange("p (n h w) -> p n h w", n=4, h=4)
            STATE3 = STATE[:].rearrange("p (h w) -> p h w", h=4)
            STBF3 = STBF[:].rearrange("p (h w) -> p h w", h=4)
            GSIG3 = YT[:, 8 * T:16 * T].rearrange("p (n f) -> p n f", n=4)
            YT3 = YT[:].rearrange("p (k t) -> p k t", k=KC)
            ACTB3 = ACTB[:].rearrange("p (k t) -> p k t", k=FC)
            ER3 = ER[:].rearrange("p (n c) -> p n c", n=4)
            vpb, stateb, stbfb, qkhb, mcurb, gsb, erb, albb = (Buf(n) for n in ("vp", "state", "stbf", "qkh", "mcur", "gsmall", "er", "alb"))
            ktokb, sptb, hmb, smallb = (Buf(n) for n in ("ktok", "spt", "hm", "small"))
            QSC = float(DQK) ** -0.5

            def GS(i):
                return YT[0:4, i * T:(i + 1) * T]

            def mixer_fn(b, ti):
                first = (ti % TPS == 0)
                if first:
                    P.op("dve", lambda e: e.memset(STATE[:], 0.0), writes=[stateb])
                    P.op("dve", lambda e: e.memset(MCUR[:], 0.0), writes=[mcurb])
                    P.op("dve", lambda e: e.memset(QKH[:], 0.0), writes=[qkhb])
                P.op("pe", [mm(PS[6][0:4, :], WG3[:, kc, 0:4], HTc(kc), kc == 0, kc == KC - 1) for kc in range(KC)], reads=ht + mc, writes=[psb[6]])
                P.op("pe", [mm(PS[7][0:4, :], WG3[:, kc, 4:8], HTc(kc), kc == 0, kc == KC - 1) for kc in range(KC)], reads=ht + mc, writes=[psb[7]])
                P.op("act", lambda e: e.activation(out=GS(0), in_=PS[6][0:4, :], func=AF.Identity, bias=BIG[:, 0:1], scale=1.0), reads=[psb[6]] + mc, writes=[yt[0]])
                P.op("act", lambda e: e.activation(out=GS(1), in_=PS[7][0:4, :], func=AF.Exp, bias=NBFG[:, 0:1], scale=-1.0), reads=[psb[7]] + mc, writes=[yt[1]])
                P.op("act", lambda e: e.activation(out=GS(1), in_=GS(1), func=AF.Ln, bias=1.0, scale=1.0), reads=[yt[1]], writes=[yt[1]])
                P.op("dve", lambda e: e.tensor_tensor_scan(out=GS(2), data0=RESETM[:], data1=GS(1), initial=0.0, op0=ALU.mult, op1=ALU.add),
                     reads=[yt[1]] + mc, writes=[yt[2]])
                P.op("dve", lambda e: e.tensor_tensor(out=GS(3), in0=GS(0), in1=GS(2), op=ALU.add), reads=[yt[0], yt[2]], writes=[yt[3]])
                G3 = GS(3).rearrange("p (c l) -> p c l", l=128)
                NB3 = GS(2).rearrange("p (c l) -> p c l", l=128)
                P.op("dve", lambda e: e.tensor_reduce(out=GMAX[:], in_=G3, axis=AX.X, op=ALU.max), reads=[yt[3]], writes=[gsb])
                for n in range(NCH):
                    P.op("dve", lambda e, n=n: e.tensor_tensor(out=MM_[:, n:n + 1], in0=MCUR[:], in1=GMAX[:, n:n + 1], op=ALU.max), reads=[gsb, mcurb], writes=[gsb])
                    P.op("dve", lambda e, n=n: e.tensor_tensor(out=AL[:, n:n + 1], in0=MCUR[:], in1=MM_[:, n:n + 1], op=ALU.subtract), reads=[gsb, mcurb], writes=[gsb])
                    P.op("dve", lambda e, n=n: e.tensor_tensor(out=MCUR[:], in0=MM_[:, n:n + 1], in1=NB3[:, n, 127:128], op=ALU.subtract), reads=[gsb, yt[2]], writes=[mcurb])
                MMB = MM_[:].unsqueeze(2).to_broadcast([4, 4, 128])
                E3 = GS(4).rearrange("p (c l) -> p c l", l=128)
                R3 = GS(5).rearrange("p (c l) -> p c l", l=128)
                P.op("dve", lambda e: e.tensor_tensor(out=E3, in0=G3, in1=MMB, op=ALU.subtract), reads=[yt[3], gsb], writes=[yt[4]])
                P.op("dve", lambda e: e.tensor_tensor(out=R3, in0=NB3, in1=MMB, op=ALU.subtract), reads=[yt[2], gsb], writes=[yt[5]])
                P.op("act", lambda e: e.activation(out=GS(4), in_=GS(4), func=AF.Exp), reads=[yt[4]], writes=[yt[4]])
                P.op("act", lambda e: e.activation(out=GS(5), in_=GS(5), func=AF.Exp), reads=[yt[5]], writes=[yt[5]])
                P.op("act", lambda e: e.activation(out=ALPHA[:], in_=AL[:], func=AF.Exp), reads=[gsb], writes=[gsb])
                for n in range(NCH):
                    P.op("pe", mm(PS[5][:, n * 8:n * 8 + 4], GS(4)[:, n * 128:(n + 1) * 128], IDENTF[0:4, 0:4], True, True), reads=[yt[4]] + mc, writes=[psb[5]])
                    P.op("pe", mm(PS[5][:, n * 8 + 4:n * 8 + 8], GS(5)[:, n * 128:(n + 1) * 128], IDENTF[0:4, 0:4], True, True), reads=[yt[5]] + mc, writes=[psb[5]])
                ALD3 = ALD[:].rearrange("p (h n) -> p h n", h=4)
                P.op("dve", lambda e: e.tensor_tensor(out=ALD3, in0=IDENTF[0:4, 0:4].unsqueeze(2).to_broadcast([4, 4, 4]),
                                                      in1=ALPHA[:].unsqueeze(1).to_broadcast([4, 4, 4]), op=ALU.mult), reads=[gsb] + mc, writes=[gsb])
                P.op("pe", mm(PS[5][:, 32:48], ONESF[:, :], ALD[:], True, True), reads=[gsb] + mc, writes=[psb[5]])
                P.op("dve", lambda e: e.tensor_copy(out=ER[:], in_=PS[5][:, 0:32]), reads=[psb[5]], writes=[erb])
                P.op("dve", lambda e: e.tensor_copy(out=ALB[:], in_=PS[5][:, 32:48]), reads=[psb[5]], writes=[albb])
                if mstage < 2:
                    return
                P.op("dve", lambda e: e.tensor_copy(out=QKP3[:, :, 0:3], in_=QKH3), reads=[qkhb], writes=[scrA])
                for blk in range(2):
                    si = ring_load(win_d[blk], KC * 512)
                    wv = RING[si][:, 0:KC * 512].rearrange("p (k c) -> p k c", k=KC)
                    for j in range(4):
                        bank = 1 + j % 2
                        ch = blk * 4 + j
                        P.op("pe", [mm(PS[bank][:, :], wv[:, kc, j * 128:(j + 1) * 128], HTc(kc), kc == 0, kc == KC - 1) for kc in range(KC)],
                             reads=[ringb[si]] + ht, writes=[psb[bank]])
                        P.op("act", lambda e, ch=ch, bank=bank: e.activation(out=QKP3[:, ch, 3:3 + T], in_=PS[bank][:, :], func=AF.Identity),
                             reads=[psb[bank]], writes=[scrA])
                for ch in range(8):
                    P.op("dve", lambda e, ch=ch: e.tensor_scalar(out=SG[0][:], in0=QKP3[:, ch, 0:T], scalar1=CONVW[:, ch * 4:ch * 4 + 1], scalar2=None, op0=ALU.mult),
                         reads=[scrA] + mc, writes=[sgb[0]])
                    for tap in range(1, 4):
                        P.op("dve", lambda e, ch=ch, tap=tap: e.scalar_tensor_tensor(out=SG[0][:], in0=QKP3[:, ch, tap:tap + T],
                                                                                   scalar=CONVW[:, ch * 4 + tap:ch * 4 + tap + 1], in1=SG[0][:],
                                                                                   op0=ALU.mult, op1=ALU.add), reads=[scrA, sgb[0]] + mc, writes=[sgb[0]])
                    P.op("act", lambda e, ch=ch: e.activation(out=ACc(16 + ch), in_=SG[0][:], func=AF.Silu, bias=CONVB[:, ch:ch + 1], scale=1.0),
                         reads=[sgb[0]] + mc, writes=[actb[16 + ch]])
                P.op("dve", lambda e: e.tensor_copy(out=QKH3, in_=QKP3[:, :, T:T + 3]), reads=[scrA], writes=[qkhb])
                if mstage < 3:
                    return
                for blk in range(2, 4):
                    si = ring_load(win_d[blk], KC * 512)
                    wv = RING[si][:, 0:KC * 512].rearrange("p (k c) -> p k c", k=KC)
                    for n in range(NCH):
                        bank = 1 + n % 2
                        P.op("pe", [mm(PS[bank][:, :], HTc(kc)[:, n * 128:(n + 1) * 128], wv[:, kc, :], kc == 0, kc == KC - 1) for kc in range(KC)],
                             reads=[ringb[si]] + ht, writes=[psb[bank]])
                        for hh in range(2):
                            h = (blk - 2) * 2 + hh
                            P.op("dve", lambda e, n=n, h=h, hh=hh, bank=bank: e.tensor_scalar(out=VP4[:, n, h, 0:256], in0=PS[bank][:, hh * 256:(hh + 1) * 256],
                                                                                               scalar1=ER[:, n * 8 + h:n * 8 + h + 1], scalar2=None, op0=ALU.mult),
                                 reads=[psb[bank], erb], writes=[vpb])
                P.op("dve", lambda e: e.tensor_copy(out=VP4[:, :, :, 256], in_=ER3[:, :, 0:4]), reads=[erb], writes=[vpb])
                if mstage < 4:
                    return
                for blk in range(4, 6):
                    half = blk - 4
                    si = ring_load(win_d[blk], KC * 512)
                    wv = RING[si][:, 0:KC * 512].rearrange("p (k c) -> p k c", k=KC)
                    for n in range(NCH):
                        bank = 1 + n % 2
                        P.op("pe", [mm(PS[bank][:, :], HTc(kc)[:, n * 128:(n + 1) * 128], wv[:, kc, :], kc == 0, kc == KC - 1) for kc in range(KC)],
                             reads=[ringb[si]] + ht, writes=[psb[bank]])
                        P.op("act", lambda e, bank=bank: e.activation(out=SG[1][:], in_=PS[bank][:, :], func=AF.Sigmoid), reads=[psb[bank]], writes=[sgb[1]])
                        P.op("dve", lambda e, n=n, half=half: e.tensor_tensor(out=GSIG3[:, n, half * 512:(half + 1) * 512], in0=SG[1][:],
                                                                             in1=GMHB[:, half * 512:(half + 1) * 512], op=ALU.mult),
                             reads=[sgb[1]] + mc, writes=[yt[8 + 2 * n + half]])
                if mstage < 5:
                    return
                for blk in range(6, 8):
                    si = ring_load(win_d[blk], KC * 512)
                    wv = RING[si][:, 0:KC * 512].rearrange("p (k c) -> p k c", k=KC)
                    for j in range(4):
                        bank = 1 + j % 2
                        ch = (blk - 6) * 4 + j
                        P.op("pe", [mm(PS[bank][:, :], wv[:, kc, j * 128:(j + 1) * 128], HTc(kc), kc == 0, kc == KC - 1) for kc in range(KC)],
                             reads=[ringb[si]] + ht, writes=[psb[bank]])
                        P.op("act", lambda e, ch=ch, bank=bank: e.activation(out=YTc(ch), in_=PS[bank][:, :], func=AF.Gelu_apprx_tanh),
                             reads=[psb[bank]], writes=[yt[ch]])
                if mstage < 6:
                    return
                for blk in range(8, 10):
                    half = blk - 8
                    si = ring_load(win_d[blk], KC * 512)
                    wv = RING[si][:, 0:KC * 512].rearrange("p (k c) -> p k c", k=KC)
                    for n in range(NCH):
                        bank = 1 + n % 2
                        P.op("pe", [mm(PS[bank][:, :], HTc(kc)[:, n * 128:(n + 1) * 128], wv[:, kc, :], kc == 0, kc == KC - 1) for kc in range(KC)],
                             reads=[ringb[si]] + ht, writes=[psb[bank]])
                        P.op("act", lambda e, n=n, half=half, bank=bank: e.activation(out=GG3[:, n, half * 512:(half + 1) * 512], in_=PS[bank][:, :],
                                                                                       func=AF.Gelu_apprx_tanh), reads=[psb[bank]], writes=[scrA])
                if mstage < 7:
                    return
                for n in range(dbg_nch):
                    cs = slice(n * 128, (n + 1) * 128)
                    VN = HT[:, 2 * n * T:(2 * n + 2) * T]
                    vnb = [ht[2 * n], ht[2 * n + 1]]
                    P.op("dve", lambda e: e.memset(SMALL[:, 0:2], 0.0), writes=[smallb])
                    P.op("act", lambda e, n=n, VN=VN: e.activation(out=VN, in_=GG3[:, n, :], func=AF.Identity, accum_out=SMALL[:, 0:1]), reads=[scrA, smallb], writes=vnb + [smallb])
                    P.op("act", lambda e, n=n, VN=VN: e.activation(out=VN, in_=GG3[:, n, :], func=AF.Square, accum_out=SMALL[:, 1:2]), reads=[scrA, smallb], writes=vnb + [smallb])
                    P.op("dve", lambda e: e.tensor_scalar(out=SMALL[:, 2:3], in0=SMALL[:, 0:1], scalar1=1.0 / 1024, scalar2=None, op0=ALU.mult), reads=[smallb], writes=[smallb])
                    P.op("dve", lambda e: e.tensor_tensor(out=SMALL[:, 3:4], in0=SMALL[:, 2:3], in1=SMALL[:, 2:3], op=ALU.mult), reads=[smallb], writes=[smallb])
                    P.op("dve", lambda e: e.scalar_tensor_tensor(out=SMALL[:, 4:5], in0=SMALL[:, 1:2], scalar=1.0 / 1024, in1=SMALL[:, 3:4], op0=ALU.mult, op1=ALU.subtract),
                         reads=[smallb], writes=[smallb])
                    P.op("act", lambda e: e.activation(out=SMALL[:, 5:6], in_=SMALL[:, 4:5], func=AF.Sqrt, bias=EPS, scale=1.0), reads=[smallb], writes=[smallb])
                    P.op("dve", lambda e: e.reciprocal(out=SMALL[:, 6:7], in_=SMALL[:, 5:6]), reads=[smallb], writes=[smallb])
                    P.op("dve", lambda e, n=n: e.tensor_scalar(out=GG3[:, n, :], in0=GG3[:, n, :], scalar1=SMALL[:, 2:3], scalar2=SMALL[:, 6:7], op0=ALU.subtract, op1=ALU.mult),
                         reads=[scrA, smallb], writes=[scrA])
                    P.op("dve", lambda e, n=n: e.tensor_tensor(out=GG3[:, n, :], in0=GG3[:, n, :], in1=LNGB[:], op=ALU.mult), reads=[scrA] + mc, writes=[scrA])
                    P.op("dve", lambda e, n=n, VN=VN: e.tensor_tensor(out=VN, in0=GG3[:, n, :], in1=LNBB[:], op=ALU.add), reads=[scrA] + mc, writes=vnb)
                    for g0 in (0, 4):
                        bank = 1 + g0 // 4
                        fns = []
                        for g in range(g0, g0 + 4):
                            o_ = PS[bank][:, (g - g0) * 128:(g - g0 + 1) * 128]
                            fns.append(mm(o_, VN[:, g * 128:(g + 1) * 128], WCT3[:, g, :], True, False))
                            fns.append(mm(o_, IDENTB[0:1, 0:1].to_broadcast([1, 128]) if False else ONESB[0:1, :], BSPB[0:1, g * 128:(g + 1) * 128], False, True))
                        P.op("pe", fns, reads=vnb + [wctb, onesb] + mc, writes=[psb[bank]])
                        P.op("dve", lambda e, g0=g0, bank=bank, cs=cs: e.tensor_tensor(out=ACTB3[:, 8 + g0:12 + g0, cs], in0=PS[bank][:, :].rearrange("p (g t) -> p g t", g=4),
                                                                                        in1=YT3[:, g0:g0 + 4, cs], op=ALU.mult),
                             reads=[psb[bank]] + [yt[g0 + i] for i in range(4)], writes=[actb[8 + g0 + i] for i in range(4)])
                    P.op("pe", [(lambda e, h=h, cs=cs: e.transpose(PSB[:, h * 128:(h + 1) * 128], ACc(20 + h)[:, cs], IDENTB[:])) for h in range(4)],
                         reads=[actb[20 + h] for h in range(4)] + mc, writes=[psb[0]])
                    P.op("act", lambda e: e.activation(out=KTOK[:], in_=PSB[:, 0:512], func=AF.Identity, scale=QSC), reads=[psb[0]], writes=[ktokb])
                    P.op("pe", [mm(PS[1][:, h * 128:(h + 1) * 128], ACc(20 + h)[:, cs], ACc(16 + h)[:, cs], True, True) for h in range(4)],
                         reads=[actb[16 + h] for h in range(8)], writes=[psb[1]])
                    P.op("dve", lambda e: e.scalar_tensor_tensor(out=SPT[:], in0=PS[1][:, :], scalar=QSC, in1=MASK4[:], op0=ALU.mult, op1=ALU.mult),
                         reads=[psb[1]] + mc, writes=[sptb])
                    for h in range(4):
                        P.op("dve", lambda e, h=h, n=n: e.tensor_scalar(out=STBF3[:, h, 0:257], in0=STATE3[:, h, :], scalar1=ALB[:, h * 4 + n:h * 4 + n + 1], scalar2=None, op0=ALU.mult),
                             reads=[stateb, albb], writes=[stbfb])
                    for h in range(4):
                        P.op("pe", [mm(PS[2 + h][:, 0:257], ACc(16 + h)[:, cs], STBF3[:, h, 0:257], True, False),
                                    mm(PS[2 + h][:, 0:257], SPT[:, h * 128:(h + 1) * 128], VP4[:, n, h, 0:257], False, True)],
                             reads=[actb[16 + h], stbfb, sptb, vpb], writes=[psb[2 + h]])
                    for h in range(4):
                        bank = 6 + h % 2
                        P.op("pe", mm(PS[bank][:, 0:257], KTOK[:, h * 128:(h + 1) * 128], VP4[:, n, h, 0:257], True, True), reads=[ktokb, vpb], writes=[psb[bank]])
                        P.op("dve", lambda e, h=h, n=n, bank=bank: e.scalar_tensor_tensor(out=STATE3[:, h, :], in0=STATE3[:, h, :], scalar=ALB[:, h * 4 + n:h * 4 + n + 1],
                                                                                           in1=PS[bank][:, 0:257], op0=ALU.mult, op1=ALU.add),
                             reads=[stateb, albb, psb[bank]], writes=[stateb])
                    P.op("dve", lambda e: e.memset(SMALL[:, 16:20], 0.0), writes=[smallb])
                    for h in range(4):
                        P.op("act", lambda e, h=h: e.activation(out=SMALL[:, 8 + h:9 + h], in_=PS[2 + h][:, 256:257], func=AF.Identity), reads=[psb[2 + h]], writes=[smallb])
                        P.op("act", lambda e, h=h: e.activation(out=SG[1][:, 0:256], in_=PS[2 + h][:, 0:256], func=AF.Square, accum_out=SMALL[:, 16 + h:17 + h]),
                             reads=[psb[2 + h], smallb], writes=[sgb[1], smallb])
                    DEN, NEG, RDEN, SS, T1_, RS, SCL = (SMALL[:, 8:12], SMALL[:, 12:16], SMALL[:, 20:24], SMALL[:, 16:20], SMALL[:, 24:28], SMALL[:, 28:32], SMALL[:, 32:36])
                    P.op("dve", lambda e: e.tensor_scalar(out=NEG, in0=DEN, scalar1=-1.0, scalar2=None, op0=ALU.mult), reads=[smallb], writes=[smallb])
                    P.op("dve", lambda e: e.tensor_tensor(out=NEG, in0=NEG, in1=DEN, op=ALU.max), reads=[smallb], writes=[smallb])
                    P.op("dve", lambda e, n=n: e.tensor_tensor(out=NEG, in0=NEG, in1=ER[:, n * 8 + 4:n * 8 + 8], op=ALU.max), reads=[smallb, erb], writes=[smallb])
                    P.op("dve", lambda e: e.reciprocal(out=RDEN, in_=NEG), reads=[smallb], writes=[smallb])
                    P.op("dve", lambda e: e.tensor_tensor(out=T1_, in0=RDEN, in1=RDEN, op=ALU.mult), reads=[smallb], writes=[smallb])
                    P.op("dve", lambda e: e.tensor_tensor(out=T1_, in0=T1_, in1=SS, op=ALU.mult), reads=[smallb], writes=[smallb])
                    P.op("act", lambda e: e.activation(out=RS, in_=T1_, func=AF.Sqrt, bias=EPS, scale=1.0 / DV), reads=[smallb], writes=[smallb])
                    P.op("dve", lambda e: e.reciprocal(out=RS, in_=RS), reads=[smallb], writes=[smallb])
                    P.op("dve", lambda e: e.tensor_tensor(out=SCL, in0=RS, in1=RDEN, op=ALU.mult), reads=[smallb], writes=[smallb])
                    for h in range(4):
                        P.op("dve", lambda e, h=h, n=n: e.scalar_tensor_tensor(out=HM[:, h * 256:(h + 1) * 256], in0=PS[2 + h][:, 0:256], scalar=SMALL[:, 32 + h:33 + h],
                                                                               in1=GSIG3[:, n, h * 256:(h + 1) * 256], op0=ALU.mult, op1=ALU.mult),
                             reads=[psb[2 + h], smallb, yt[8 + 2 * n], yt[9 + 2 * n]], writes=[hmb])
                    P.op("pe", [(lambda e, j=j: e.transpose(PSB[:, j * 128:(j + 1) * 128], HM[:, j * 128:(j + 1) * 128], IDENTB[:])) for j in range(8)],
                         reads=[hmb] + mc, writes=[psb[0]])
                    P.op("act", lambda e, cs=cs: e.activation(out=ACTB3[:, 0:8, cs], in_=PSB[:, :].rearrange("p (j t) -> p j t", j=8), func=AF.Identity),
                         reads=[psb[0]], writes=[actb[j] for j in range(8)])
                if mstage < 8:
                    return
                for blk in range(WOUT_BLK):
                    si = ring_load(wout_d[blk], KC * 512)
                    wv = RING[si][:, 0:KC * 512].rearrange("p (k c) -> p k c", k=KC)
                    for j in range(4):
                        d = blk * 4 + j
                        bank = 5 + d % 2
                        P.op("pe", [mm(PS[bank][:, :], wv[:, kc, j * 128:(j + 1) * 128], ACc(kc), kc == 0, kc == KC - 1) for kc in range(KC)],
                             reads=[ringb[si]] + [actb[k] for k in range(KC)], writes=[psb[bank]])
                        evac_y(d, bank)

        xT_v = xT_d.rearrange("(k p) t -> p k t", p=128)
        out_v = out_d.rearrange("(k p) t -> p k t", p=128)
        XT3 = XT[:].rearrange("p (k t) -> p k t", k=KC)
        last_store = None
        for ti in range(ntiles):
            b = ti // TPS
            tok0 = ti * T
            for kc in range(KC):
                P.dma("sp", cxin, XT3[:, kc, :], xT_v[:, kc, tok0:tok0 + T], writes=[xt[kc]])
            for kc in range(KC):
                xt[kc].w = (cxin.key, 16 * cxin.n, "dma")
            for j in range(3):
                if j == 1:
                    if not mixer:
                        continue
                    prenorm(b, j)
                    mixer_fn(b, ti)
                    postnorm(b, j)
                else:
                    if not do_ffn:
                        continue
                    prenorm(b, j)
                    if stage >= 2:
                        ffn(0 if j == 0 else 1)
                    if stage >= 3:
                        postnorm(b, j)
            for kc in range(KC):
                P.dma("sp", cxout, out_v[:, kc, tok0:tok0 + T], XT3[:, kc, :], reads=[xt[kc]])
            last_store = (cxout.key, 16 * cxout.n, "dma")
            for kc in range(KC):
                xt[kc].r = [last_store]
        P.wait_event("sp", last_store)

        for name in P.semnames:
            P.sems[name] = es.enter_context(nc.semaphore(name))
        block = es.enter_context(nc.Block())

        def replay(key):
            def f(e):
                for fn in P.q[key]:
                    fn(e)
            return f

        block.tensor(replay("pe"))
        block.scalar(replay("act"))
        block.vector(replay("dve"))
        block.gpsimd(replay("pool"))
        block.sync(replay("sp"))
    print("instr counts:", {k: len(v) for k, v in P.q.items()})
    nc._prog_log = P.log
    return nc


def _blk(w, nblk, cols):
    K = w.shape[0]
    kc = K // 128
    return np.ascontiguousarray(w.reshape(kc, 128, nblk, cols).transpose(2, 1, 0, 3)).reshape(nblk, 128, kc * cols)


def prep_shared(inp):
    f = np.float32
    sh = {}
    sh["wada"] = _blk(inp["w_ada"][0], ADA_BLK, 512)
    sh["bada"] = np.ascontiguousarray(inp["b_ada"][0].reshape(144, 128).T)
    sh["gpre"] = np.ascontiguousarray(inp["g_pre"][0].reshape(3, KC, 128).transpose(2, 0, 1)).reshape(128, 3 * KC)
    sh["gpost"] = np.ascontiguousarray(inp["g_post"][0].reshape(3, KC, 128).transpose(2, 0, 1)).reshape(128, 3 * KC)
    for i in range(2):
        g = _blk(inp["w_ff_gate"][0, i], GU_BLK, 256).reshape(GU_BLK, 128, 1, KC * 256)
        u = _blk(inp["w_ff_up"][0, i], GU_BLK, 256).reshape(GU_BLK, 128, 1, KC * 256)
        sh["gu%d" % i] = np.ascontiguousarray(np.concatenate([g, u], axis=2)).reshape(GU_BLK, 128, 2 * KC * 256)
        sh["dn%d" % i] = _blk(inp["w_ff_down"][0, i], DN_BLK, 128)
    win = inp["w_in"][0]
    cols = np.concatenate([np.arange(0, 3072), np.arange(3080, 5128)])
    sh["win"] = _blk(np.ascontiguousarray(win[:, cols]), WIN_BLK, 512)
    sh["wgate"] = np.ascontiguousarray(win[:, 3072:3080].reshape(KC, 128, 8).transpose(1, 0, 2)).reshape(128, KC * 8)
    sh["wout"] = _blk(inp["w_out"][0], WOUT_BLK, 512)
    sh["convw"] = np.ascontiguousarray(inp["conv_w"][0].reshape(4, 8, 128).transpose(2, 1, 0)).reshape(128, 32)
    sh["convb"] = np.ascontiguousarray(inp["conv_b"][0].reshape(8, 128).T)
    sh["big"] = np.ascontiguousarray(inp["b_igate"][0].reshape(4, 1))
    sh["bfg"] = np.ascontiguousarray(inp["b_fgate"][0].reshape(4, 1))
    sh["gmh"] = np.ascontiguousarray(inp["g_mhnorm"][0].reshape(1, 1024))
    sh["lng"] = np.ascontiguousarray(inp["gmlp_ln_g"][0].reshape(1, 1024))
    sh["lnb"] = np.ascontiguousarray(inp["gmlp_ln_b"][0].reshape(1, 1024))
    sh["wsp"] = np.ascontiguousarray(inp["w_spatial"][0].transpose(1, 0, 2)).reshape(128, 1024)
    sh["bsp"] = np.ascontiguousarray(inp["b_spatial"][0].reshape(1, 1024))
    sh["ident"] = np.eye(128, dtype=f)
    tril = np.tril(np.ones((128, 128), dtype=f))
    sh["tril"] = tril
    sh["mask4"] = np.ascontiguousarray(np.tile(tril.T, (1, 4)))
    rm = np.ones((4, 512), dtype=f); rm[:, ::128] = 0.0
    sh["resetm"] = rm
    sh["gmhbc"] = np.ascontiguousarray(np.broadcast_to(sh["gmh"], (128, 1024)))
    sh["lngbc"] = np.ascontiguousarray(np.broadcast_to(sh["lng"], (128, 1024)))
    sh["lnbbc"] = np.ascontiguousarray(np.broadcast_to(sh["lnb"], (128, 1024)))
    return {k: np.asarray(v, dtype=f) for k, v in sh.items()}


def make_in_maps(inp, n_cores=8):
    inp = {k: np.asarray(v) for k, v in inp.items()}
    sh = prep_shared(inp)
    maps = []
    for c in range(n_cores):
        m = dict(sh)
        xs = inp["x"][c * NSEQ:(c + 1) * NSEQ].reshape(NTOK, D)
        m["xT"] = np.ascontiguousarray(xs.T)
        cs = inp["c"][c * NSEQ:(c + 1) * NSEQ]
        m["c_l"] = np.ascontiguousarray(cs.reshape(NSEQ, KC, 128).transpose(2, 1, 0))
        maps.append(m)
    return maps


def kernel(**inputs):
    n = 8
    maps = make_in_maps(inputs, n)
    nc = build_nc()
    res = run_bass_kernel_spmd(nc, maps, core_ids=list(range(n)))
    outs = []
    for c in range(n):
        oT = np.asarray(res.results[c]["outT"])
        outs.append(np.ascontiguousarray(oT.T).reshape(NSEQ, S, D))
    return np.concatenate(outs, axis=0).astype(np.float32)
```

```python
import contextlib
import numpy as np
import concourse.bass as bass
import concourse.mybir as mybir
from concourse.bass_utils import run_bass_kernel_spmd

F32 = mybir.dt.float32
BF16 = mybir.dt.bfloat16
AF = mybir.ActivationFunctionType
ALU = mybir.AluOpType
AX = mybir.AxisListType

D = 2048
KC = 16
S = 2048
T = 512
NCH = T // 128
DFF = 5632
FC = 44
NSEQ = 2
NTOK = NSEQ * S
NT = NTOK // T
TPS = S // T
EPS = 1e-6
NH = 4
DQK = 128
DV = 256
VW = 258
GU_BLK = 22
DN_BLK = 16
ADA_BLK = 36
WIN_BLK = 10
WOUT_BLK = 4
SLOT = 8192
NSLOT = 2
RING_PIECE = 8192

ENGS = ("pe", "act", "dve", "pool", "sp")


class Buf:
    __slots__ = ("name", "w", "r")

    def __init__(self, name):
        self.name = name
        self.w = None
        self.r = []


class Chan:
    def __init__(self, key):
        self.key = key
        self.n = 0


class Prog:
    def __init__(self):
        self.q = {e: [] for e in ENGS}
        self.cnt = {e: 0 for e in ENGS}
        self.waited = {e: {} for e in ENGS}
        self.semnames = ["E_" + e for e in ENGS]
        self.sems = {}
        self.log = []

    def chan(self, name):
        key = "C_" + name
        self.semnames.append(key)
        return Chan(key)

    def _need(self, eng, ev, raw):
        sem, val, origin = ev
        if origin == eng and not raw:
            return
        if self.waited[eng].get(sem, 0) >= val:
            return
        self.waited[eng][sem] = val
        self.log.append((eng, "wait", sem, val))
        self.q[eng].append(lambda e, s=sem, v=val: e.wait_ge(self.sems[s], v))

    def _deps(self, eng, reads, writes):
        for b in reads:
            if b.w is not None:
                self._need(eng, b.w, True)
        for b in writes:
            if b.w is not None:
                self._need(eng, b.w, False)
            for ev in b.r:
                self._need(eng, ev, False)

    def op(self, eng, fns, reads=(), writes=()):
        if not isinstance(fns, (list, tuple)):
            fns = [fns]
        self._deps(eng, reads, writes)
        self.cnt[eng] += 1
        key = "E_" + eng
        ev = (key, self.cnt[eng], eng)
        self.log.append((eng, "op", self.cnt[eng], [b.name for b in reads], [b.name for b in writes]))
        for f in fns[:-1]:
            self.q[eng].append(f)
        last = fns[-1]
        self.q[eng].append(lambda e, f=last, k=key: f(e).then_inc(self.sems[k], 1))
        for b in reads:
            b.r.append(ev)
        for b in writes:
            b.w = ev
            b.r = []
        return ev

    def dma(self, eng, chan, out_ap, in_ap, reads=(), writes=()):
        self._deps(eng, reads, writes)
        chan.n += 1
        ev = (chan.key, 16 * chan.n, "dma")
        self.q[eng].append(
            lambda e, o=out_ap, i=in_ap, k=chan.key: e.dma_start(out=o, in_=i).then_inc(self.sems[k], 16)
        )
        for b in reads:
            b.r.append(ev)
        for b in writes:
            b.w = ev
            b.r = []
        return ev

    def wait_event(self, eng, ev):
        self._need(eng, ev, True)


def build_nc(mixer=True, do_ffn=True, ntiles=NT, ada=True, stage=3, dbg_nch=NCH, mstage=8, use_cache=True):
    nc = bass.Bass("TRN2", target_bir_lowering=False)
    P = Prog()

    def din(name, shape):
        return nc.dram_tensor(name, list(shape), F32, kind="ExternalInput").ap()

    xT_d = din("xT", (D, NTOK))
    c_d = din("c_l", (128, KC, NSEQ))
    wada_d = din("wada", (ADA_BLK, 128, KC * 512))
    bada_d = din("bada", (128, 144))
    gpre_d = din("gpre", (128, 3 * KC))
    gpost_d = din("gpost", (128, 3 * KC))
    gu_d = [din("gu%d" % i, (GU_BLK, 128, 2 * KC * 256)) for i in range(2)]
    dn_d = [din("dn%d" % i, (DN_BLK, 128, FC * 128)) for i in range(2)]
    win_d = din("win", (WIN_BLK, 128, KC * 512))
    wgate_d = din("wgate", (128, KC * 8))
    wout_d = din("wout", (WOUT_BLK, 128, KC * 512))
    convw_d = din("convw", (128, 8 * 4))
    convb_d = din("convb", (128, 8))
    big_d = din("big", (4, 1))
    bfg_d = din("bfg", (4, 1))
    gmh_d = din("gmh", (1, 1024))
    lng_d = din("lng", (1, 1024))
    lnb_d = din("lnb", (1, 1024))
    wsp_d = din("wsp", (128, 8 * 128))
    bsp_d = din("bsp", (1, 1024))
    ident_d = din("ident", (128, 128))
    mask4_d = din("mask4", (128, 512))
    tril_d = din("tril", (128, 128))
    reset_d = din("resetm", (4, 512))
    gmhbc_d = din("gmhbc", (128, 1024))
    lngbc_d = din("lngbc", (128, 1024))
    lnbbc_d = din("lnbbc", (128, 1024))
    out_d = nc.dram_tensor("outT", [D, NTOK], F32, kind="ExternalOutput").ap()

    es = contextlib.ExitStack()
    with es:
        def sb(name, shape, dt=F32):
            return es.enter_context(nc.sbuf_tensor(name, list(shape), dt))

        def psum(name, shape, dt=F32):
            return es.enter_context(nc.psum_tensor(name, list(shape), dt))

        XT = sb("XT", [128, KC * T])
        HT = sb("HT", [128, KC * T], BF16)
        ACTB = sb("ACTB", [128, FC * T], BF16)
        YT = sb("YT", [128, KC * T])
        RING = [sb("RING%d" % i, [128, SLOT], BF16) for i in range(NSLOT)]
        SG = [sb("SG%d" % i, [128, T]) for i in range(2)]
        RSTD = sb("RSTD", [128, T])
        RT = sb("RT", [128, T])
        ONESB = sb("ONESB", [128, 128], BF16)
        MODT = sb("MODT", [128, 144 * NSEQ])
        BADA = sb("BADA", [128, 144])
        GPRE = sb("GPRE", [128, 3 * KC])
        GPOST = sb("GPOST", [128, 3 * KC])
        PA = sb("PA", [128, NSEQ * 3 * KC])
        PSH = sb("PSH", [128, NSEQ * 3 * KC])
        PCG = sb("PCG", [128, NSEQ * 3 * KC])
        CL = sb("CL", [128, KC * NSEQ])
        SCT = sb("SCT", [128, KC * NSEQ], BF16)

        if mixer:
            WGATE = sb("WGATE", [128, KC * 8], BF16)
            CONVW = sb("CONVW", [128, 32]); CONVB = sb("CONVB", [128, 8])
            BIG = sb("BIG", [4, 1]); BFG = sb("BFG", [4, 1]); NBFG = sb("NBFG", [4, 1])
            GMHB = sb("GMHB", [128, 1024], BF16); LNGB = sb("LNGB", [128, 1024], BF16); LNBB = sb("LNBB", [128, 1024], BF16)
            WCT = sb("WCT", [128, 1024], BF16)
            BSPB = sb("BSPB", [1, 1024], BF16)
            ONESF = sb("ONESF", [4, 128])
            MASK4 = sb("MASK4", [128, 512], BF16)
            TRIL = sb("TRIL", [128, 128])
            IDENTB = sb("IDENTB", [128, 128], BF16); IDENTF = sb("IDENTF", [128, 128])
            RESETM = sb("RESETM", [4, 512])
            VP = sb("VP", [128, 16 * VW], BF16)
            STATE = sb("STATE", [128, 4 * 257]); STBF = sb("STBF", [128, 4 * VW], BF16)
            QKH = sb("QKH", [128, 24])
            MCUR = sb("MCUR", [4, 1]); MM_ = sb("MM", [4, 4]); AL = sb("AL", [4, 4]); ALPHA = sb("ALPHA", [4, 4]); GMAX = sb("GMAX", [4, 4])
            ALD = sb("ALD", [4, 16]); ER = sb("ER", [128, 32]); ALB = sb("ALB", [128, 16])
            KTOK = sb("KTOK", [128, 512], BF16); SPT = sb("SPT", [128, 512], BF16); HM = sb("HM", [128, 1024], BF16)
            SMALL = sb("SMALL", [128, 64])
        PSB = psum("PSB", [128, 1024], BF16)
        PS = [None] + [psum("PS%d" % i, [128, 512]) for i in range(1, 8)]

        xt = [Buf("xt%d" % i) for i in range(KC)]
        ht = [Buf("ht%d" % i) for i in range(KC)]
        actb = [Buf("actb%d" % i) for i in range(FC)]
        yt = [Buf("yt%d" % i) for i in range(KC)]
        ringb = [Buf("ring%d" % i) for i in range(NSLOT)]
        ringc = [P.chan("ring%d" % i) for i in range(NSLOT)]
        sgb = [Buf("sg0"), Buf("sg1")]
        rstdb, rtb = Buf("rstd"), Buf("rt")
        psb = [Buf("ps%d" % i) for i in range(8)]
        parb = Buf("params")
        cpar = P.chan("par")
        cxin = P.chan("xin")
        cxout = P.chan("xout")
        ring_i = [0]

        def XTc(k):
            return XT[:, k * T:(k + 1) * T]

        def HTc(k):
            return HT[:, k * T:(k + 1) * T]

        def ACc(k):
            return ACTB[:, k * T:(k + 1) * T]

        def YTc(k):
            return YT[:, k * T:(k + 1) * T]

        stc = [P.chan("st%d" % i) for i in range(NSLOT)]
        ringh = [P.chan("ringh%d" % i) for i in range(NSLOT)]
        cacheb = {}

        def ring_load(src_ap, nelem, cache=None):
            i = ring_i[0] % NSLOT
            ring_i[0] += 1
            if cache is not None:
                cache = (src_ap.bitcast(BF16)[:, 0:nelem], cache[1])
            if cache is not None and cache[1] in cacheb:
                P.dma("sp", ringh[i], RING[i][:, 0:nelem], cache[0], reads=[cacheb[cache[1]]], writes=[ringb[i]])
                return i
            for o in range(0, nelem, RING_PIECE):
                n = min(RING_PIECE, nelem - o)
                P.dma("pool", ringc[i], RING[i][:, o:o + n], src_ap[:, o:o + n], writes=[ringb[i]] if o == 0 else [])
            ringb[i].w = (ringc[i].key, 16 * ringc[i].n, "dma")
            if cache is not None and use_cache:
                cb = Buf("cache_%s" % (cache[1],))
                P.dma("sp", stc[i], cache[0], RING[i][:, 0:nelem], reads=[ringb[i]], writes=[cb])
                cacheb[cache[1]] = cb
            return i

        def mm(out, lhsT, rhs, start, stop):
            return lambda e: e.matmul(out, lhsT, rhs, start=start, stop=stop)

        par_loads = [(CL[:], c_d.rearrange("p k b -> p (k b)")), (BADA[:], bada_d), (GPRE[:], gpre_d), (GPOST[:], gpost_d)]
        for o, i in par_loads:
            P.dma("sp", cpar, o, i, writes=[])
        parb.w = (cpar.key, 16 * cpar.n, "dma")
        onesb = Buf("ones")
        P.op("dve", lambda e: e.memset(ONESB[:], 1.0), writes=[onesb])
        sctb = Buf("sct")
        P.op("act", lambda e: e.activation(out=SCT[:], in_=CL[:], func=AF.Silu), reads=[parb], writes=[sctb])
        SCT3 = SCT[:].rearrange("p (k b) -> p k b", b=NSEQ)
        for blk in range(ADA_BLK if ada else 0):
            si = ring_load(wada_d[blk], KC * 512)
            wv = RING[si][:, 0:KC * 512].rearrange("p (k c) -> p k c", k=KC)
            for j in range(4):
                fchunk = blk * 4 + j
                fns = [mm(PS[7][:, fchunk * NSEQ:(fchunk + 1) * NSEQ], wv[:, kc, j * 128:(j + 1) * 128], SCT3[:, kc, :], kc == 0, kc == KC - 1)
                       for kc in range(KC)]
                P.op("pe", fns, reads=[ringb[si], sctb], writes=[psb[7]])
        modb = Buf("modt")
        MODT3 = MODT[:].rearrange("p (m b) -> p m b", b=NSEQ)
        P.op("dve", lambda e: e.tensor_tensor(out=MODT3, in0=PS[7][:, 0:144 * NSEQ].rearrange("p (m b) -> p m b", b=NSEQ),
                                              in1=BADA[:].unsqueeze(2).to_broadcast([128, 144, NSEQ]), op=ALU.add),
             reads=[psb[7], parb], writes=[modb])
        derb = Buf("derived")
        coefs = [0.5, 1.0, 0.5]
        for b in range(NSEQ):
            for j in range(3):
                o = (b * 3 + j) * KC
                sh = MODT3[:, (j * 3 + 0) * KC:(j * 3 + 1) * KC, b]
                sc = MODT3[:, (j * 3 + 1) * KC:(j * 3 + 2) * KC, b]
                gt = MODT3[:, (j * 3 + 2) * KC:(j * 3 + 3) * KC, b]
                P.op("dve", lambda e, o=o, sc=sc, j=j: e.scalar_tensor_tensor(out=PA[:, o:o + KC], in0=sc, scalar=1.0, in1=GPRE[:, j * KC:(j + 1) * KC],
                                                                          op0=ALU.add, op1=ALU.mult), reads=[modb, parb], writes=[derb])
                P.op("dve", lambda e, o=o, sh=sh: e.tensor_copy(out=PSH[:, o:o + KC], in_=sh), reads=[modb], writes=[derb])
                P.op("dve", lambda e, o=o, gt=gt, j=j: e.scalar_tensor_tensor(out=PCG[:, o:o + KC], in0=gt, scalar=coefs[j], in1=GPOST[:, j * KC:(j + 1) * KC],
                                                                          op0=ALU.mult, op1=ALU.mult), reads=[modb, parb], writes=[derb])

        def rstd_from_sq():
            fns = [mm(PS[7][:, :], ONESB[:, :], HTc(kc), kc == 0, kc == KC - 1) for kc in range(KC)]
            P.op("pe", fns, reads=ht + [onesb], writes=[psb[7]])
            P.op("act", lambda e: e.activation(out=RT[:], in_=PS[7][:, :], func=AF.Sqrt, bias=EPS, scale=1.0 / D), reads=[psb[7]], writes=[rtb])
            P.op("dve", lambda e: e.reciprocal(out=RSTD[:], in_=RT[:]), reads=[rtb], writes=[rstdb])

        def prenorm(b, j):
            o = (b * 3 + j) * KC
            for kc in range(KC):
                P.op("act", lambda e, kc=kc: e.activation(out=HTc(kc), in_=XTc(kc), func=AF.Square), reads=[xt[kc]], writes=[ht[kc]])
            rstd_from_sq()
            for kc in range(KC):
                P.op("dve", lambda e, kc=kc: e.scalar_tensor_tensor(out=YTc(kc), in0=XTc(kc), scalar=PA[:, o + kc:o + kc + 1], in1=RSTD[:],
                                                                    op0=ALU.mult, op1=ALU.mult), reads=[xt[kc], rstdb, derb], writes=[yt[kc]])
                P.op("act", lambda e, kc=kc: e.activation(out=HTc(kc), in_=YTc(kc), func=AF.Identity, bias=PSH[:, o + kc:o + kc + 1], scale=1.0),
                     reads=[yt[kc], derb], writes=[ht[kc]])

        def postnorm(b, j):
            o = (b * 3 + j) * KC
            rstd_from_sq()
            for kc in range(KC):
                P.op("dve", lambda e, kc=kc: e.scalar_tensor_tensor(out=YTc(kc), in0=YTc(kc), scalar=PCG[:, o + kc:o + kc + 1], in1=RSTD[:],
                                                                    op0=ALU.mult, op1=ALU.mult), reads=[yt[kc], rstdb, derb], writes=[yt[kc]])
                P.op("dve", lambda e, kc=kc: e.tensor_tensor(out=XTc(kc), in0=XTc(kc), in1=YTc(kc), op=ALU.add), reads=[xt[kc], yt[kc]], writes=[xt[kc]])

        def evac_y(dchunk, bank):
            P.op("dve", lambda e: e.tensor_copy(out=YTc(dchunk), in_=PS[bank][:, :]), reads=[psb[bank]], writes=[yt[dchunk]])
            P.op("act", lambda e: e.activation(out=HTc(dchunk), in_=YTc(dchunk), func=AF.Square), reads=[yt[dchunk]], writes=[ht[dchunk]])

        def ffn(idx):
            for blk in range(GU_BLK):
                si = ring_load(gu_d[idx][blk], 2 * KC * 256, (None, ('gu', idx, blk)))
                wv = RING[si][:, 0:2 * KC * 256].rearrange("p (g k c) -> p g k c", g=2, k=KC)
                for j in range(2):
                    f = blk * 2 + j
                    bg, bu = 1 + (f % 2) * 2, 2 + (f % 2) * 2
                    for g, bank in ((0, bg), (1, bu)):
                        fns = [mm(PS[bank][:, :], wv[:, g, kc, j * 128:(j + 1) * 128], HTc(kc), kc == 0, kc == KC - 1) for kc in range(KC)]
                        P.op("pe", fns, reads=[ringb[si]] + ht, writes=[psb[bank]])
                    s = f % 2
                    P.op("act", lambda e, s=s, bg=bg: e.activation(out=SG[s][:], in_=PS[bg][:, :], func=AF.Silu), reads=[psb[bg]], writes=[sgb[s]])
                    P.op("dve", lambda e, s=s, bu=bu, f=f: e.tensor_tensor(out=ACc(f), in0=PS[bu][:, :], in1=SG[s][:], op=ALU.mult),
                         reads=[psb[bu], sgb[s]], writes=[actb[f]])
            for blk in range(DN_BLK if stage >= 3 else 0):
                si = ring_load(dn_d[idx][blk], FC * 128, (None, ('dn', idx, blk)))
                wv = RING[si][:, 0:FC * 128].rearrange("p (k c) -> p k c", k=FC)
                bank = 5 + blk % 2
                fns = [mm(PS[bank][:, :], wv[:, kc, :], ACc(kc), kc == 0, kc == FC - 1) for kc in range(FC)]
                P.op("pe", fns, reads=[ringb[si]] + actb, writes=[psb[bank]])
                evac_y(blk, bank)


        if mixer:
            cpar2 = P.chan("par2")
            cpar3 = P.chan("par3")
            mconst = Buf("mconst")
            for o, i in [(CONVW[:], convw_d), (CONVB[:], convb_d), (BIG[:], big_d), (BFG[:], bfg_d), (TRIL[:], tril_d), (IDENTF[:], ident_d),
                         (RESETM[:], reset_d), (YT[:, 0:1024], wsp_d)]:
                P.dma("sp", cpar3, o, i, writes=[])
            mconst.w = (cpar3.key, 16 * cpar3.n, "dma")
            for kc in range(2):
                yt[kc].w = mconst.w
            for o, i in [(WGATE[:], wgate_d), (GMHB[:], gmhbc_d), (LNGB[:], lngbc_d), (LNBB[:], lnbbc_d), (BSPB[:], bsp_d), (MASK4[:], mask4_d),
                         (IDENTB[:], ident_d)]:
                P.dma("pool", cpar2, o, i, writes=[])
            mconst2 = Buf("mconst2")
            mconst2.w = (cpar2.key, 16 * cpar2.n, "dma")
            mc = [mconst, mconst2]
            P.op("dve", lambda e: e.memset(ONESF[:], 1.0), writes=[mconst])
            P.op("dve", lambda e: e.tensor_scalar(out=NBFG[:], in0=BFG[:], scalar1=-1.0, scalar2=None, op0=ALU.mult), reads=[mconst], writes=[mconst])
            wctb = Buf("wct")
            WSP3 = YT[:, 0:1024].rearrange("p (g s) -> p g s", g=8)
            P.op("dve", lambda e: e.tensor_tensor(out=WSP3, in0=WSP3, in1=TRIL[:].unsqueeze(1).to_broadcast([128, 8, 128]), op=ALU.mult),
                 reads=[yt[0], yt[1], mconst], writes=[yt[0], yt[1]])
            for g in range(8):
                bank = 1 + g // 4
                P.op("pe", mm(PS[bank][:, (g % 4) * 128:(g % 4 + 1) * 128], WSP3[:, g, :], IDENTF[:], True, True), reads=[yt[0], yt[1], mconst], writes=[psb[bank]])
            for hb in range(2):
                P.op("act", lambda e, hb=hb: e.activation(out=WCT[:, hb * 512:(hb + 1) * 512], in_=PS[1 + hb][:, :], func=AF.Identity),
                     reads=[psb[1 + hb]], writes=[wctb])
            WCT3 = WCT[:].rearrange("p (g t) -> p g t", g=8)
            WG3 = WGATE[:].rearrange("p (k c) -> p k c", k=KC)
            scrA = Buf("scrA")
            QKP3 = ACTB[:, 24 * T:24 * T + 8240].bitcast(F32).rearrange("p (j t) -> p j t", j=8)
            GG3 = ACTB[:, 24 * T:24 * T + 8192].bitcast(F32).rearrange("p (n f) -> p n f", n=4)
            QKH3 = QKH[:].rearrange("p (j t) -> p j t", j=8)
            VP4 = VP[:].rearrange("p (n h w) -> p n h w", n=4, h=4)
            STATE3 = STATE[:].rearrange("p (h w) -> p h w", h=4)
            STBF3 = STBF[:].rearrange("p (h w) -> p h w", h=4)
            GSIG3 = YT[:, 8 * T:16 * T].rearrange("p (n f) -> p n f", n=4)
            YT3 = YT[:].rearrange("p (k t) -> p k t", k=KC)
            ACTB3 = ACTB[:].rearrange("p (k t) -> p k t", k=FC)
            ER3 = ER[:].rearrange("p (n c) -> p n c", n=4)
            vpb, stateb, stbfb, qkhb, mcurb, gsb, erb, albb = (Buf(n) for n in ("vp", "state", "stbf", "qkh", "mcur", "gsmall", "er", "alb"))
            ktokb, sptb, hmb, smallb = (Buf(n) for n in ("ktok", "spt", "hm", "small"))
            QSC = float(DQK) ** -0.5

            def GS(i):
                return YT[0:4, i * T:(i + 1) * T]

            def mixer_fn(b, ti):
                first = (ti % TPS == 0)
                if first:
                    P.op("dve", lambda e: e.memset(STATE[:], 0.0), writes=[stateb])
                    P.op("dve", lambda e: e.memset(MCUR[:], 0.0), writes=[mcurb])
                    P.op("dve", lambda e: e.memset(QKH[:], 0.0), writes=[qkhb])
                P.op("pe", [mm(PS[6][0:4, :], WG3[:, kc, 0:4], HTc(kc), kc == 0, kc == KC - 1) for kc in range(KC)], reads=ht + mc, writes=[psb[6]])
                P.op("pe", [mm(PS[7][0:4, :], WG3[:, kc, 4:8], HTc(kc), kc == 0, kc == KC - 1) for kc in range(KC)], reads=ht + mc, writes=[psb[7]])
                P.op("act", lambda e: e.activation(out=GS(0), in_=PS[6][0:4, :], func=AF.Identity, bias=BIG[:, 0:1], scale=1.0), reads=[psb[6]] + mc, writes=[yt[0]])
                P.op("act", lambda e: e.activation(out=GS(1), in_=PS[7][0:4, :], func=AF.Exp, bias=NBFG[:, 0:1], scale=-1.0), reads=[psb[7]] + mc, writes=[yt[1]])
                P.op("act", lambda e: e.activation(out=GS(1), in_=GS(1), func=AF.Ln, bias=1.0, scale=1.0), reads=[yt[1]], writes=[yt[1]])
                P.op("dve", lambda e: e.tensor_tensor_scan(out=GS(2), data0=RESETM[:], data1=GS(1), initial=0.0, op0=ALU.mult, op1=ALU.add),
                     reads=[yt[1]] + mc, writes=[yt[2]])
                P.op("dve", lambda e: e.tensor_tensor(out=GS(3), in0=GS(0), in1=GS(2), op=ALU.add), reads=[yt[0], yt[2]], writes=[yt[3]])
                G3 = GS(3).rearrange("p (c l) -> p c l", l=128)
                NB3 = GS(2).rearrange("p (c l) -> p c l", l=128)
                P.op("dve", lambda e: e.tensor_reduce(out=GMAX[:], in_=G3, axis=AX.X, op=ALU.max), reads=[yt[3]], writes=[gsb])
                for n in range(NCH):
                    P.op("dve", lambda e, n=n: e.tensor_tensor(out=MM_[:, n:n + 1], in0=MCUR[:], in1=GMAX[:, n:n + 1], op=ALU.max), reads=[gsb, mcurb], writes=[gsb])
                    P.op("dve", lambda e, n=n: e.tensor_tensor(out=AL[:, n:n + 1], in0=MCUR[:], in1=MM_[:, n:n + 1], op=ALU.subtract), reads=[gsb, mcurb], writes=[gsb])
                    P.op("dve", lambda e, n=n: e.tensor_tensor(out=MCUR[:], in0=MM_[:, n:n + 1], in1=NB3[:, n, 127:128], op=ALU.subtract), reads=[gsb, yt[2]], writes=[mcurb])
                MMB = MM_[:].unsqueeze(2).to_broadcast([4, 4, 128])
                E3 = GS(4).rearrange("p (c l) -> p c l", l=128)
                R3 = GS(5).rearrange("p (c l) -> p c l", l=128)
                P.op("dve", lambda e: e.tensor_tensor(out=E3, in0=G3, in1=MMB, op=ALU.subtract), reads=[yt[3], gsb], writes=[yt[4]])
                P.op("dve", lambda e: e.tensor_tensor(out=R3, in0=NB3, in1=MMB, op=ALU.subtract), reads=[yt[2], gsb], writes=[yt[5]])
                P.op("act", lambda e: e.activation(out=GS(4), in_=GS(4), func=AF.Exp), reads=[yt[4]], writes=[yt[4]])
                P.op("act", lambda e: e.activation(out=GS(5), in_=GS(5), func=AF.Exp), reads=[yt[5]], writes=[yt[5]])
                P.op("act", lambda e: e.activation(out=ALPHA[:], in_=AL[:], func=AF.Exp), reads=[gsb], writes=[gsb])
                for n in range(NCH):
                    P.op("pe", mm(PS[5][:, n * 8:n * 8 + 4], GS(4)[:, n * 128:(n + 1) * 128], IDENTF[0:4, 0:4], True, True), reads=[yt[4]] + mc, writes=[psb[5]])
                    P.op("pe", mm(PS[5][:, n * 8 + 4:n * 8 + 8], GS(5)[:, n * 128:(n + 1) * 128], IDENTF[0:4, 0:4], True, True), reads=[yt[5]] + mc, writes=[psb[5]])
                ALD3 = ALD[:].rearrange("p (h n) -> p h n", h=4)
                P.op("dve", lambda e: e.tensor_tensor(out=ALD3, in0=IDENTF[0:4, 0:4].unsqueeze(2).to_broadcast([4, 4, 4]),
                                                      in1=ALPHA[:].unsqueeze(1).to_broadcast([4, 4, 4]), op=ALU.mult), reads=[gsb] + mc, writes=[gsb])
                P.op("pe", mm(PS[5][:, 32:48], ONESF[:, :], ALD[:], True, True), reads=[gsb] + mc, writes=[psb[5]])
                P.op("dve", lambda e: e.tensor_copy(out=ER[:], in_=PS[5][:, 0:32]), reads=[psb[5]], writes=[erb])
                P.op("dve", lambda e: e.tensor_copy(out=ALB[:], in_=PS[5][:, 32:48]), reads=[psb[5]], writes=[albb])
                if mstage < 2:
                    return
                P.op("dve", lambda e: e.tensor_copy(out=QKP3[:, :, 0:3], in_=QKH3), reads=[qkhb], writes=[scrA])
                for blk in range(2):
                    si = ring_load(win_d[blk], KC * 512, (None, ('win', blk)))
                    wv = RING[si][:, 0:KC * 512].rearrange("p (k c) -> p k c", k=KC)
                    for j in range(4):
                        bank = 1 + j % 2
                        ch = blk * 4 + j
                        P.op("pe", [mm(PS[bank][:, :], wv[:, kc, j * 128:(j + 1) * 128], HTc(kc), kc == 0, kc == KC - 1) for kc in range(KC)],
                             reads=[ringb[si]] + ht, writes=[psb[bank]])
                        P.op("act", lambda e, ch=ch, bank=bank: e.activation(out=QKP3[:, ch, 3:3 + T], in_=PS[bank][:, :], func=AF.Identity),
                             reads=[psb[bank]], writes=[scrA])
                for ch in range(8):
                    P.op("dve", lambda e, ch=ch: e.tensor_scalar(out=SG[0][:], in0=QKP3[:, ch, 0:T], scalar1=CONVW[:, ch * 4:ch * 4 + 1], scalar2=None, op0=ALU.mult),
                         reads=[scrA] + mc, writes=[sgb[0]])
                    for tap in range(1, 4):
                        P.op("dve", lambda e, ch=ch, tap=tap: e.scalar_tensor_tensor(out=SG[0][:], in0=QKP3[:, ch, tap:tap + T],
                                                                                   scalar=CONVW[:, ch * 4 + tap:ch * 4 + tap + 1], in1=SG[0][:],
                                                                                   op0=ALU.mult, op1=ALU.add), reads=[scrA, sgb[0]] + mc, writes=[sgb[0]])
                    P.op("act", lambda e, ch=ch: e.activation(out=ACc(16 + ch), in_=SG[0][:], func=AF.Silu, bias=CONVB[:, ch:ch + 1], scale=1.0),
                         reads=[sgb[0]] + mc, writes=[actb[16 + ch]])
                P.op("dve", lambda e: e.tensor_copy(out=QKH3, in_=QKP3[:, :, T:T + 3]), reads=[scrA], writes=[qkhb])
                if mstage < 3:
                    return
                for blk in range(2, 4):
                    si = ring_load(win_d[blk], KC * 512, (None, ('win', blk)))
                    wv = RING[si][:, 0:KC * 512].rearrange("p (k c) -> p k c", k=KC)
                    for n in range(NCH):
                        bank = 1 + n % 2
                        P.op("pe", [mm(PS[bank][:, :], HTc(kc)[:, n * 128:(n + 1) * 128], wv[:, kc, :], kc == 0, kc == KC - 1) for kc in range(KC)],
                             reads=[ringb[si]] + ht, writes=[psb[bank]])
                        for hh in range(2):
                            h = (blk - 2) * 2 + hh
                            P.op("dve", lambda e, n=n, h=h, hh=hh, bank=bank: e.tensor_scalar(out=VP4[:, n, h, 0:256], in0=PS[bank][:, hh * 256:(hh + 1) * 256],
                                                                                               scalar1=ER[:, n * 8 + h:n * 8 + h + 1], scalar2=None, op0=ALU.mult),
                                 reads=[psb[bank], erb], writes=[vpb])
                P.op("dve", lambda e: e.tensor_copy(out=VP4[:, :, :, 256], in_=ER3[:, :, 0:4]), reads=[erb], writes=[vpb])
                if mstage < 4:
                    return
                for blk in range(4, 6):
                    half = blk - 4
                    si = ring_load(win_d[blk], KC * 512, (None, ('win', blk)))
                    wv = RING[si][:, 0:KC * 512].rearrange("p (k c) -> p k c", k=KC)
                    for n in range(NCH):
                        bank = 1 + n % 2
                        P.op("pe", [mm(PS[bank][:, :], HTc(kc)[:, n * 128:(n + 1) * 128], wv[:, kc, :], kc == 0, kc == KC - 1) for kc in range(KC)],
                             reads=[ringb[si]] + ht, writes=[psb[bank]])
                        P.op("act", lambda e, bank=bank: e.activation(out=SG[1][:], in_=PS[bank][:, :], func=AF.Sigmoid), reads=[psb[bank]], writes=[sgb[1]])
                        P.op("dve", lambda e, n=n, half=half: e.tensor_tensor(out=GSIG3[:, n, half * 512:(half + 1) * 512], in0=SG[1][:],
                                                                             in1=GMHB[:, half * 512:(half + 1) * 512], op=ALU.mult),
                             reads=[sgb[1]] + mc, writes=[yt[8 + 2 * n + half]])
                if mstage < 5:
                    return
                for blk in range(6, 8):
                    si = ring_load(win_d[blk], KC * 512, (None, ('win', blk)))
                    wv = RING[si][:, 0:KC * 512].rearrange("p (k c) -> p k c", k=KC)
                    for j in range(4):
                        bank = 1 + j % 2
                        ch = (blk - 6) * 4 + j
                        P.op("pe", [mm(PS[bank][:, :], wv[:, kc, j * 128:(j + 1) * 128], HTc(kc), kc == 0, kc == KC - 1) for kc in range(KC)],
                             reads=[ringb[si]] + ht, writes=[psb[bank]])
                        P.op("act", lambda e, ch=ch, bank=bank: e.activation(out=YTc(ch), in_=PS[bank][:, :], func=AF.Gelu_apprx_tanh),
                             reads=[psb[bank]], writes=[yt[ch]])
                if mstage < 6:
                    return
                for blk in range(8, 10):
                    half = blk - 8
                    si = ring_load(win_d[blk], KC * 512, (None, ('win', blk)))
                    wv = RING[si][:, 0:KC * 512].rearrange("p (k c) -> p k c", k=KC)
                    for n in range(NCH):
                        bank = 1 + n % 2
                        P.op("pe", [mm(PS[bank][:, :], HTc(kc)[:, n * 128:(n + 1) * 128], wv[:, kc, :], kc == 0, kc == KC - 1) for kc in range(KC)],
                             reads=[ringb[si]] + ht, writes=[psb[bank]])
                        P.op("act", lambda e, n=n, half=half, bank=bank: e.activation(out=GG3[:, n, half * 512:(half + 1) * 512], in_=PS[bank][:, :],
                                                                                       func=AF.Gelu_apprx_tanh), reads=[psb[bank]], writes=[scrA])
                if mstage < 7:
                    return
                for n in range(dbg_nch):
                    cs = slice(n * 128, (n + 1) * 128)
                    VN = HT[:, 2 * n * T:(2 * n + 2) * T]
                    vnb = [ht[2 * n], ht[2 * n + 1]]
                    P.op("dve", lambda e: e.memset(SMALL[:, 0:2], 0.0), writes=[smallb])
                    P.op("act", lambda e, n=n, VN=VN: e.activation(out=VN, in_=GG3[:, n, :], func=AF.Identity, accum_out=SMALL[:, 0:1]), reads=[scrA, smallb], writes=vnb + [smallb])
                    P.op("act", lambda e, n=n, VN=VN: e.activation(out=VN, in_=GG3[:, n, :], func=AF.Square, accum_out=SMALL[:, 1:2]), reads=[scrA, smallb], writes=vnb + [smallb])
                    P.op("dve", lambda e: e.tensor_scalar(out=SMALL[:, 2:3], in0=SMALL[:, 0:1], scalar1=1.0 / 1024, scalar2=None, op0=ALU.mult), reads=[smallb], writes=[smallb])
                    P.op("dve", lambda e: e.tensor_tensor(out=SMALL[:, 3:4], in0=SMALL[:, 2:3], in1=SMALL[:, 2:3], op=ALU.mult), reads=[smallb], writes=[smallb])
                    P.op("dve", lambda e: e.scalar_tensor_tensor(out=SMALL[:, 4:5], in0=SMALL[:, 1:2], scalar=1.0 / 1024, in1=SMALL[:, 3:4], op0=ALU.mult, op1=ALU.subtract),
                         reads=[smallb], writes=[smallb])
                    P.op("act", lambda e: e.activation(out=SMALL[:, 5:6], in_=SMALL[:, 4:5], func=AF.Sqrt, bias=EPS, scale=1.0), reads=[smallb], writes=[smallb])
                    P.op("dve", lambda e: e.reciprocal(out=SMALL[:, 6:7], in_=SMALL[:, 5:6]), reads=[smallb], writes=[smallb])
                    P.op("dve", lambda e, n=n: e.tensor_scalar(out=GG3[:, n, :], in0=GG3[:, n, :], scalar1=SMALL[:, 2:3], scalar2=SMALL[:, 6:7], op0=ALU.subtract, op1=ALU.mult),
                         reads=[scrA, smallb], writes=[scrA])
                    P.op("dve", lambda e, n=n: e.tensor_tensor(out=GG3[:, n, :], in0=GG3[:, n, :], in1=LNGB[:], op=ALU.mult), reads=[scrA] + mc, writes=[scrA])
                    P.op("dve", lambda e, n=n, VN=VN: e.tensor_tensor(out=VN, in0=GG3[:, n, :], in1=LNBB[:], op=ALU.add), reads=[scrA] + mc, writes=vnb)
                    for g0 in (0, 4):
                        bank = 1 + g0 // 4
                        fns = []
                        for g in range(g0, g0 + 4):
                            o_ = PS[bank][:, (g - g0) * 128:(g - g0 + 1) * 128]
                            fns.append(mm(o_, VN[:, g * 128:(g + 1) * 128], WCT3[:, g, :], True, False))
                            fns.append(mm(o_, IDENTB[0:1, 0:1].to_broadcast([1, 128]) if False else ONESB[0:1, :], BSPB[0:1, g * 128:(g + 1) * 128], False, True))
                        P.op("pe", fns, reads=vnb + [wctb, onesb] + mc, writes=[psb[bank]])
                        P.op("dve", lambda e, g0=g0, bank=bank, cs=cs: e.tensor_tensor(out=ACTB3[:, 8 + g0:12 + g0, cs], in0=PS[bank][:, :].rearrange("p (g t) -> p g t", g=4),
                                                                                        in1=YT3[:, g0:g0 + 4, cs], op=ALU.mult),
                             reads=[psb[bank]] + [yt[g0 + i] for i in range(4)], writes=[actb[8 + g0 + i] for i in range(4)])
                    P.op("pe", [(lambda e, h=h, cs=cs: e.transpose(PSB[:, h * 128:(h + 1) * 128], ACc(20 + h)[:, cs], IDENTB[:])) for h in range(4)],
                         reads=[actb[20 + h] for h in range(4)] + mc, writes=[psb[0]])
                    P.op("act", lambda e: e.activation(out=KTOK[:], in_=PSB[:, 0:512], func=AF.Identity, scale=QSC), reads=[psb[0]], writes=[ktokb])
                    P.op("pe", [mm(PS[1][:, h * 128:(h + 1) * 128], ACc(20 + h)[:, cs], ACc(16 + h)[:, cs], True, True) for h in range(4)],
                         reads=[actb[16 + h] for h in range(8)], writes=[psb[1]])
                    P.op("dve", lambda e: e.scalar_tensor_tensor(out=SPT[:], in0=PS[1][:, :], scalar=QSC, in1=MASK4[:], op0=ALU.mult, op1=ALU.mult),
                         reads=[psb[1]] + mc, writes=[sptb])
                    for h in range(4):
                        P.op("dve", lambda e, h=h, n=n: e.tensor_scalar(out=STBF3[:, h, 0:257], in0=STATE3[:, h, :], scalar1=ALB[:, h * 4 + n:h * 4 + n + 1], scalar2=None, op0=ALU.mult),
                             reads=[stateb, albb], writes=[stbfb])
                    for h in range(4):
                        P.op("pe", [mm(PS[2 + h][:, 0:257], ACc(16 + h)[:, cs], STBF3[:, h, 0:257], True, False),
                                    mm(PS[2 + h][:, 0:257], SPT[:, h * 128:(h + 1) * 128], VP4[:, n, h, 0:257], False, True)],
                             reads=[actb[16 + h], stbfb, sptb, vpb], writes=[psb[2 + h]])
                    for h in range(4):
                        bank = 6 + h % 2
                        P.op("pe", mm(PS[bank][:, 0:257], KTOK[:, h * 128:(h + 1) * 128], VP4[:, n, h, 0:257], True, True), reads=[ktokb, vpb], writes=[psb[bank]])
                        P.op("dve", lambda e, h=h, n=n, bank=bank: e.scalar_tensor_tensor(out=STATE3[:, h, :], in0=STATE3[:, h, :], scalar=ALB[:, h * 4 + n:h * 4 + n + 1],
                                                                                           in1=PS[bank][:, 0:257], op0=ALU.mult, op1=ALU.add),
                             reads=[stateb, albb, psb[bank]], writes=[stateb])
                    P.op("dve", lambda e: e.memset(SMALL[:, 16:20], 0.0), writes=[smallb])
                    for h in range(4):
                        P.op("act", lambda e, h=h: e.activation(out=SMALL[:, 8 + h:9 + h], in_=PS[2 + h][:, 256:257], func=AF.Identity), reads=[psb[2 + h]], writes=[smallb])
                        P.op("act", lambda e, h=h: e.activation(out=SG[1][:, 0:256], in_=PS[2 + h][:, 0:256], func=AF.Square, accum_out=SMALL[:, 16 + h:17 + h]),
                             reads=[psb[2 + h], smallb], writes=[sgb[1], smallb])
                    DEN, NEG, RDEN, SS, T1_, RS, SCL = (SMALL[:, 8:12], SMALL[:, 12:16], SMALL[:, 20:24], SMALL[:, 16:20], SMALL[:, 24:28], SMALL[:, 28:32], SMALL[:, 32:36])
                    P.op("dve", lambda e: e.tensor_scalar(out=NEG, in0=DEN, scalar1=-1.0, scalar2=None, op0=ALU.mult), reads=[smallb], writes=[smallb])
                    P.op("dve", lambda e: e.tensor_tensor(out=NEG, in0=NEG, in1=DEN, op=ALU.max), reads=[smallb], writes=[smallb])
                    P.op("dve", lambda e, n=n: e.tensor_tensor(out=NEG, in0=NEG, in1=ER[:, n * 8 + 4:n * 8 + 8], op=ALU.max), reads=[smallb, erb], writes=[smallb])
                    P.op("dve", lambda e: e.reciprocal(out=RDEN, in_=NEG), reads=[smallb], writes=[smallb])
                    P.op("dve", lambda e: e.tensor_tensor(out=T1_, in0=RDEN, in1=RDEN, op=ALU.mult), reads=[smallb], writes=[smallb])
                    P.op("dve", lambda e: e.tensor_tensor(out=T1_, in0=T1_, in1=SS, op=ALU.mult), reads=[smallb], writes=[smallb])
                    P.op("act", lambda e: e.activation(out=RS, in_=T1_, func=AF.Sqrt, bias=EPS, scale=1.0 / DV), reads=[smallb], writes=[smallb])
                    P.op("dve", lambda e: e.reciprocal(out=RS, in_=RS), reads=[smallb], writes=[smallb])
                    P.op("dve", lambda e: e.tensor_tensor(out=SCL, in0=RS, in1=RDEN, op=ALU.mult), reads=[smallb], writes=[smallb])
                    for h in range(4):
                        P.op("dve", lambda e, h=h, n=n: e.scalar_tensor_tensor(out=HM[:, h * 256:(h + 1) * 256], in0=PS[2 + h][:, 0:256], scalar=SMALL[:, 32 + h:33 + h],
                                                                               in1=GSIG3[:, n, h * 256:(h + 1) * 256], op0=ALU.mult, op1=ALU.mult),
                             reads=[psb[2 + h], smallb, yt[8 + 2 * n], yt[9 + 2 * n]], writes=[hmb])
                    P.op("pe", [(lambda e, j=j: e.transpose(PSB[:, j * 128:(j + 1) * 128], HM[:, j * 128:(j + 1) * 128], IDENTB[:])) for j in range(8)],
                         reads=[hmb] + mc, writes=[psb[0]])
                    P.op("act", lambda e, cs=cs: e.activation(out=ACTB3[:, 0:8, cs], in_=PSB[:, :].rearrange("p (j t) -> p j t", j=8), func=AF.Identity),
                         reads=[psb[0]], writes=[actb[j] for j in range(8)])
                if mstage < 8:
                    return
                for blk in range(WOUT_BLK):
                    si = ring_load(wout_d[blk], KC * 512, (None, ('wout', blk)))
                    wv = RING[si][:, 0:KC * 512].rearrange("p (k c) -> p k c", k=KC)
                    for j in range(4):
                        d = blk * 4 + j
                        bank = 5 + d % 2
                        P.op("pe", [mm(PS[bank][:, :], wv[:, kc, j * 128:(j + 1) * 128], ACc(kc), kc == 0, kc == KC - 1) for kc in range(KC)],
                             reads=[ringb[si]] + [actb[k] for k in range(KC)], writes=[psb[bank]])
                        evac_y(d, bank)

        xT_v = xT_d.rearrange("(k p) t -> p k t", p=128)
        out_v = out_d.rearrange("(k p) t -> p k t", p=128)
        XT3 = XT[:].rearrange("p (k t) -> p k t", k=KC)
        last_store = None
        for ti in range(ntiles):
            b = ti // TPS
            tok0 = ti * T
            for kc in range(KC):
                P.dma("sp", cxin, XT3[:, kc, :], xT_v[:, kc, tok0:tok0 + T], writes=[xt[kc]])
            for kc in range(KC):
                xt[kc].w = (cxin.key, 16 * cxin.n, "dma")
            for j in range(3):
                if j == 1:
                    if not mixer:
                        continue
                    prenorm(b, j)
                    mixer_fn(b, ti)
                    postnorm(b, j)
                else:
                    if not do_ffn:
                        continue
                    prenorm(b, j)
                    if stage >= 2:
                        ffn(0 if j == 0 else 1)
                    if stage >= 3:
                        postnorm(b, j)
            for kc in range(KC):
                P.dma("sp", cxout, out_v[:, kc, tok0:tok0 + T], XT3[:, kc, :], reads=[xt[kc]])
            last_store = (cxout.key, 16 * cxout.n, "dma")
            for kc in range(KC):
                xt[kc].r = [last_store]
        P.wait_event("sp", last_store)

        for name in P.semnames:
            P.sems[name] = es.enter_context(nc.semaphore(name))
        block = es.enter_context(nc.Block())

        def replay(key):
            def f(e):
                for fn in P.q[key]:
                    fn(e)
            return f

        block.tensor(replay("pe"))
        block.scalar(replay("act"))
        block.vector(replay("dve"))
        block.gpsimd(replay("pool"))
        block.sync(replay("sp"))
    print("instr counts:", {k: len(v) for k, v in P.q.items()})
    nc._prog_log = P.log
    return nc


def _blk(w, nblk, cols):
    K = w.shape[0]
    kc = K // 128
    return np.ascontiguousarray(w.reshape(kc, 128, nblk, cols).transpose(2, 1, 0, 3)).reshape(nblk, 128, kc * cols)


def prep_shared(inp):
    f = np.float32
    sh = {}
    sh["wada"] = _blk(inp["w_ada"][0], ADA_BLK, 512)
    sh["bada"] = np.ascontiguousarray(inp["b_ada"][0].reshape(144, 128).T)
    sh["gpre"] = np.ascontiguousarray(inp["g_pre"][0].reshape(3, KC, 128).transpose(2, 0, 1)).reshape(128, 3 * KC)
    sh["gpost"] = np.ascontiguousarray(inp["g_post"][0].reshape(3, KC, 128).transpose(2, 0, 1)).reshape(128, 3 * KC)
    for i in range(2):
        g = _blk(inp["w_ff_gate"][0, i], GU_BLK, 256).reshape(GU_BLK, 128, 1, KC * 256)
        u = _blk(inp["w_ff_up"][0, i], GU_BLK, 256).reshape(GU_BLK, 128, 1, KC * 256)
        sh["gu%d" % i] = np.ascontiguousarray(np.concatenate([g, u], axis=2)).reshape(GU_BLK, 128, 2 * KC * 256)
        sh["dn%d" % i] = _blk(inp["w_ff_down"][0, i], DN_BLK, 128)
    win = inp["w_in"][0]
    cols = np.concatenate([np.arange(0, 3072), np.arange(3080, 5128)])
    sh["win"] = _blk(np.ascontiguousarray(win[:, cols]), WIN_BLK, 512)
    sh["wgate"] = np.ascontiguousarray(win[:, 3072:3080].reshape(KC, 128, 8).transpose(1, 0, 2)).reshape(128, KC * 8)
    sh["wout"] = _blk(inp["w_out"][0], WOUT_BLK, 512)
    sh["convw"] = np.ascontiguousarray(inp["conv_w"][0].reshape(4, 8, 128).transpose(2, 1, 0)).reshape(128, 32)
    sh["convb"] = np.ascontiguousarray(inp["conv_b"][0].reshape(8, 128).T)
    sh["big"] = np.ascontiguousarray(inp["b_igate"][0].reshape(4, 1))
    sh["bfg"] = np.ascontiguousarray(inp["b_fgate"][0].reshape(4, 1))
    sh["gmh"] = np.ascontiguousarray(inp["g_mhnorm"][0].reshape(1, 1024))
    sh["lng"] = np.ascontiguousarray(inp["gmlp_ln_g"][0].reshape(1, 1024))
    sh["lnb"] = np.ascontiguousarray(inp["gmlp_ln_b"][0].reshape(1, 1024))
    sh["wsp"] = np.ascontiguousarray(inp["w_spatial"][0].transpose(1, 0, 2)).reshape(128, 1024)
    sh["bsp"] = np.ascontiguousarray(inp["b_spatial"][0].reshape(1, 1024))
    sh["ident"] = np.eye(128, dtype=f)
    tril = np.tril(np.ones((128, 128), dtype=f))
    sh["tril"] = tril
    sh["mask4"] = np.ascontiguousarray(np.tile(tril.T, (1, 4)))
    rm = np.ones((4, 512), dtype=f); rm[:, ::128] = 0.0
    sh["resetm"] = rm
    sh["gmhbc"] = np.ascontiguousarray(np.broadcast_to(sh["gmh"], (128, 1024)))
    sh["lngbc"] = np.ascontiguousarray(np.broadcast_to(sh["lng"], (128, 1024)))
    sh["lnbbc"] = np.ascontiguousarray(np.broadcast_to(sh["lnb"], (128, 1024)))
    return {k: np.asarray(v, dtype=f) for k, v in sh.items()}


def make_in_maps(inp, n_cores=8):
    inp = {k: np.asarray(v) for k, v in inp.items()}
    sh = prep_shared(inp)
    maps = []
    for c in range(n_cores):
        m = dict(sh)
        xs = inp["x"][c * NSEQ:(c + 1) * NSEQ].reshape(NTOK, D)
        m["xT"] = np.ascontiguousarray(xs.T)
        cs = inp["c"][c * NSEQ:(c + 1) * NSEQ]
        m["c_l"] = np.ascontiguousarray(cs.reshape(NSEQ, KC, 128).transpose(2, 1, 0))
        maps.append(m)
    return maps


def kernel(**inputs):
    n = 8
    maps = make_in_maps(inputs, n)
    nc = build_nc()
    res = run_bass_kernel_spmd(nc, maps, core_ids=list(range(n)))
    outs = []
    for c in range(n):
        oT = np.asarray(res.results[c]["outT"])
        outs.append(np.ascontiguousarray(oT.T).reshape(NSEQ, S, D))
    return np.concatenate(outs, axis=0).astype(np.float32)
```
